# Optimizing a Trainium2 kernel written in Bass

```python
import jax, jax.numpy as jnp
from jax import lax
import numpy as np

D_MODEL = 1024
BATCH = 2
SEQ = 8192
DEPTH = 4

GRID_W = 64
CTX_LEN = 256
HEAD_DIM = 64
N_Q_HEADS = 8
N_KV_HEADS = 2
GROUP = N_Q_HEADS // N_KV_HEADS
ATTN_WIDTH = N_Q_HEADS * HEAD_DIM
KV_WIDTH = N_KV_HEADS * HEAD_DIM
AXIS_DIM = HEAD_DIM // 2
ROPE_THETA = 10000.0
Q_BLOCK = 128
CONV_WIDTH = D_MODEL // 4
CONV_GROUPS = 4
CONV_K = 3
CHUNK = 128
SG_GROUPS = 4
SG_WIDTH = D_MODEL // 4
N_BRANCH = 3
D_FF = -(-8 * D_MODEL // (3 * 256)) * 256
N_MOD = 6
EPS = 1e-6

OFF_Q = 3 * CONV_WIDTH
OFF_K = OFF_Q + ATTN_WIDTH
OFF_V = OFF_K + KV_WIDTH
OFF_U = OFF_V + KV_WIDTH
OFF_SV = OFF_U + SG_WIDTH
OFF_G = OFF_SV + SG_WIDTH
IN_WIDTH = OFF_G + N_BRANCH * D_MODEL

kernel_name = 'hybrid_conv_gqa_gmlp_dit_block'


def rms_norm(x, g):
    xf = x.astype(jnp.float32)
    y = xf * lax.rsqrt(jnp.mean(xf * xf, axis=-1, keepdims=True) + EPS)
    return (y * g.astype(jnp.float32)).astype(x.dtype)


def modulate(h, shift, scale):
    return h * (1 + scale) + shift


def adaln(cond, w_mod, b_mod):
    return jnp.split(jax.nn.silu(cond) @ w_mod + b_mod, N_MOD, axis=-1)


def axial_rope_tables(n):
    rows = n // GRID_W
    row = jnp.repeat(jnp.arange(rows, dtype=jnp.float32), GRID_W)
    col = jnp.tile(jnp.arange(GRID_W, dtype=jnp.float32), rows)
    inv_freq = ROPE_THETA ** (-jnp.arange(0, AXIS_DIM, 2, dtype=jnp.float32) / AXIS_DIM)
    ang = jnp.stack([row[:, None] * inv_freq, col[:, None] * inv_freq], axis=1)
    return jnp.cos(ang), jnp.sin(ang)


def apply_rope(x, cos, sin):
    xr = x.astype(jnp.float32).reshape(*x.shape[:-1], 2, 2, AXIS_DIM // 2)
    x1, x2 = xr[..., 0, :], xr[..., 1, :]
    cs, sn = cos[:, None], sin[:, None]
    out = jnp.stack([x1 * cs - x2 * sn, x2 * cs + x1 * sn], axis=-2)
    return out.reshape(x.shape).astype(x.dtype)


def short_conv(z, w):
    zp = jnp.pad(z, ((0, 0), (1, 1), (0, 0)))
    return zp[:, :-2] * w[0] + zp[:, 1:-1] * w[1] + zp[:, 2:] * w[2]


def spatial_gate(u, v, sg_norm, w_s, b_s):
    v = rms_norm(v, sg_norm)
    B, S, _ = v.shape
    vc = v.reshape(B, S // CHUNK, CHUNK, SG_GROUPS, SG_WIDTH // SG_GROUPS)
    mixed = jnp.einsum('gts,bcsgd->bctgd', w_s, vc) + b_s.T[None, None, :, :, None]
    return u * mixed.reshape(B, S, SG_WIDTH)


def gqa_attend(q, k, v):
    s = jnp.einsum('bqhgd,bkhd->bhgqk', q, k).astype(jnp.float32) * (HEAD_DIM ** -0.5)
    p = jax.nn.softmax(s, axis=-1).astype(v.dtype)
    return jnp.einsum('bhgqk,bkhd->bqhgd', p, v)


def attend_blocks(q, k, v):
    B, S = q.shape[:2]
    nb = S // Q_BLOCK
    qb = jnp.moveaxis(q.reshape(B, nb, Q_BLOCK, N_KV_HEADS, GROUP, HEAD_DIM), 1, 0)
    out = lax.map(lambda qi: gqa_attend(qi, k, v), qb)
    return jnp.moveaxis(out, 0, 1).reshape(B, S, ATTN_WIDTH)


def project(h, w_in, q_gain, k_gain):
    p = h @ w_in
    a_b, a_c, a_x, q, k, v, u, sv, g = jnp.split(
        p, (CONV_WIDTH, 2 * CONV_WIDTH, OFF_Q, OFF_K, OFF_V, OFF_U, OFF_SV, OFF_G), axis=-1)
    q = rms_norm(q.reshape(*q.shape[:-1], N_Q_HEADS, HEAD_DIM), q_gain)
    k = rms_norm(k.reshape(*k.shape[:-1], N_KV_HEADS, HEAD_DIM), k_gain)
    v = v.reshape(*v.shape[:-1], N_KV_HEADS, HEAD_DIM)
    return a_b, a_c, a_x, q, k, v, u, sv, g


def project_kv(h, w_in, k_gain):
    k, v = jnp.split(h @ w_in[:, OFF_K:OFF_U], 2, axis=-1)
    k = rms_norm(k.reshape(*k.shape[:-1], N_KV_HEADS, HEAD_DIM), k_gain)
    return k, v.reshape(*v.shape[:-1], N_KV_HEADS, HEAD_DIM)


def merge_branches(a_b, a_c, a_x, attn, u, sv, g, conv_w, sg_norm, w_s, b_s, w_a, w_b, w_c, w_o):
    y_a = (a_b * short_conv(a_c * a_x, conv_w)) @ w_a
    y_b = attn @ w_b
    y_c = spatial_gate(jax.nn.gelu(u), jax.nn.gelu(sv), sg_norm, w_s, b_s) @ w_c
    g_a, g_b, g_c = jnp.split(jax.nn.sigmoid(g), N_BRANCH, axis=-1)
    return (g_a * y_a + g_b * y_b + g_c * y_c) @ w_o


def swiglu(h, w1, w3, w2):
    return (jax.nn.silu(h @ w1) * (h @ w3)) @ w2


def setup_inputs(seed: int = 0) -> dict:
    key = jax.random.key(seed)
    ks = jax.random.split(key, 24)
    f = jnp.float32
    D = D_MODEL

    def nrm(k, shape, scale):
        return jax.random.normal(k, shape, f) * scale

    return {
        'x': nrm(ks[0], (BATCH, SEQ, D), 1.0),
        'c': nrm(ks[1], (BATCH, D), 1.0),
        'ctx': nrm(ks[2], (BATCH, CTX_LEN, D), 1.0),
        'c_ctx': nrm(ks[3], (D,), 1.0),
        'w_mod': nrm(ks[4], (DEPTH, D, N_MOD * D), 0.5 * D ** -0.5),
        'b_mod': nrm(ks[5], (DEPTH, N_MOD * D), 0.02),
        'norm1': 1.0 + nrm(ks[6], (DEPTH, D), 0.02),
        'w_in': nrm(ks[7], (DEPTH, D, IN_WIDTH), D ** -0.5),
        'q_gain': 1.0 + nrm(ks[8], (DEPTH, HEAD_DIM), 0.02),
        'k_gain': 1.0 + nrm(ks[9], (DEPTH, HEAD_DIM), 0.02),
        'conv_w': nrm(ks[10], (DEPTH, CONV_K, CONV_WIDTH), CONV_K ** -0.5),
        'sg_norm': 1.0 + nrm(ks[11], (DEPTH, SG_WIDTH), 0.02),
        'w_s': nrm(ks[12], (DEPTH, SG_GROUPS, CHUNK, CHUNK), CHUNK ** -0.5),
        'b_s': 1.0 + nrm(ks[13], (DEPTH, SG_GROUPS, CHUNK), 0.02),
        'w_a': nrm(ks[14], (DEPTH, CONV_WIDTH, D), CONV_WIDTH ** -0.5),
        'w_b': nrm(ks[15], (DEPTH, ATTN_WIDTH, D), ATTN_WIDTH ** -0.5),
        'w_c': nrm(ks[16], (DEPTH, SG_WIDTH, D), SG_WIDTH ** -0.5),
        'w_o': nrm(ks[17], (DEPTH, D, D), D ** -0.5),
        'norm2': 1.0 + nrm(ks[18], (DEPTH, D), 0.02),
        'w_ff1': nrm(ks[19], (DEPTH, D, D_FF), D ** -0.5),
        'w_ff3': nrm(ks[20], (DEPTH, D, D_FF), D ** -0.5),
        'w_ff2': nrm(ks[21], (DEPTH, D_FF, D), D_FF ** -0.5),
    }


def reference(x, c, ctx, c_ctx, w_mod, b_mod, norm1, w_in, q_gain, k_gain, conv_w, sg_norm,
              w_s, b_s, w_a, w_b, w_c, w_o, norm2, w_ff1, w_ff3, w_ff2):
    n = x.shape[1]
    cos, sin = axial_rope_tables(n)
    B, L = ctx.shape[:2]
    for l in range(DEPTH):
        last = l == DEPTH - 1
        sh1, sc1, gt1, sh2, sc2, gt2 = [m[:, None, :] for m in adaln(c, w_mod[l], b_mod[l])]
        csh1, csc1, cgt1, csh2, csc2, cgt2 = adaln(c_ctx, w_mod[l], b_mod[l])

        h_ctx = modulate(rms_norm(ctx, norm1[l]), csh1, csc1)
        if last:
            k_c, v_c = project_kv(h_ctx, w_in[l], k_gain[l])
        else:
            ca_b, ca_c, ca_x, q_c, k_c, v_c, cu, csv, cg = project(h_ctx, w_in[l], q_gain[l], k_gain[l])
            attn_c = gqa_attend(q_c.reshape(B, L, N_KV_HEADS, GROUP, HEAD_DIM), k_c, v_c)
            attn_c = attn_c.reshape(B, L, ATTN_WIDTH)

        h = modulate(rms_norm(x, norm1[l]), sh1, sc1)
        a_b, a_c, a_x, q, k, v, u, sv, g = project(h, w_in[l], q_gain[l], k_gain[l])
        q = apply_rope(q, cos, sin)
        k = apply_rope(k, cos, sin)
        k_all = jnp.concatenate([k, k_c], axis=1)
        v_all = jnp.concatenate([v, v_c], axis=1)
        attn = attend_blocks(q, k_all, v_all)
        x = x + gt1 * merge_branches(a_b, a_c, a_x, attn, u, sv, g, conv_w[l], sg_norm[l],
                                     w_s[l], b_s[l], w_a[l], w_b[l], w_c[l], w_o[l])
        x = x + gt2 * swiglu(modulate(rms_norm(x, norm2[l]), sh2, sc2), w_ff1[l], w_ff3[l], w_ff2[l])

        if not last:
            ctx = ctx + cgt1 * merge_branches(ca_b, ca_c, ca_x, attn_c, cu, csv, cg, conv_w[l],
                                              sg_norm[l], w_s[l], b_s[l], w_a[l], w_b[l],
                                              w_c[l], w_o[l])
            ctx = ctx + cgt2 * swiglu(modulate(rms_norm(ctx, norm2[l]), csh2, csc2),
                                      w_ff1[l], w_ff3[l], w_ff2[l])
    return x
```

```python
import numpy as np
from contextlib import ExitStack
import concourse.bass as bass
import concourse.mybir as mybir
from concourse.bass_utils import run_bass_kernel_spmd

F32 = mybir.dt.float32
BF16 = mybir.dt.bfloat16
AF = mybir.ActivationFunctionType
ALU = mybir.AluOpType

L = 4
D = 1024
NT = 2304
NLAT = 2048
EPS = 1e-6
BLOCKS = [(0, 512), (512, 512), (1024, 512), (1536, 512), (2048, 256)]
SLOT = 5632
NSLOT = 2
NOCC = False
WSPEC = {"WA": (8, 15), "WC2": (8, 10), "WM": (32, 8), "WO": (8, 8), "W13": (8, 44), "W2": (22, 8)}
ENGS = ["pe", "act", "dve", "pool", "sp"]


class Res:
    __slots__ = ("w", "rd", "name")

    def __init__(self, name=""):
        self.w = None
        self.rd = []
        self.name = name


class Prog:
    def __init__(self, nc, es):
        self.nc = nc
        self.es = es
        self.ops = {e: [] for e in ENGS}
        self.sems = {}
        self.cnt = {}
        self.waited = {}
        for e in ENGS:
            self.sem("e_" + e)

    def sem(self, name):
        if name not in self.sems:
            self.sems[name] = self.es.enter_context(self.nc.semaphore(name))
            self.cnt[name] = 0
        return name

    def emit(self, eng, fn, rd=(), wr=(), sig=None, inc=1, extra_deps=(), also=()):
        deps = set(extra_deps)
        for w in also:
            if w.w is not None:
                deps.add(w.w)
            deps.update(w.rd)
        for r in rd:
            if r.w is not None:
                deps.add(r.w)
        for w in wr:
            if w.w is not None:
                deps.add(w.w)
            deps.update(w.rd)
        waits = []
        for (sn, val) in sorted(deps):
            if eng == "pe" and sn == "e_pe":
                continue
            key = (eng, sn)
            if self.waited.get(key, 0) >= val:
                continue
            self.waited[key] = val
            waits.append((sn, val))
        ev = None
        if sig is not False:
            sn = sig if sig else "e_" + eng
            self.cnt[sn] += inc
            ev = (sn, self.cnt[sn])
        self.ops[eng].append((waits, fn, (ev[0], inc) if ev else None))
        if ev:
            for r in rd:
                r.rd.append(ev)
            for w in wr:
                w.w = ev
                w.rd = []
        return ev

    def barrier(self):
        evs = [("e_" + e, self.cnt["e_" + e]) for e in ["pe", "act", "dve"] if self.cnt["e_" + e] > 0]
        for e in ["pe", "act", "dve", "sp", "pool"]:
            self.emit(e, None, extra_deps=[x for x in evs if x[0] != "e_" + e], sig=False)

    def replay(self, block):
        nc = self.nc
        sems = self.sems

        def run(eng_name):
            def body(eng):
                for waits, fn, sig in self.ops[eng_name]:
                    for sn, val in waits:
                        eng.wait_ge(sems[sn], val)
                    if fn is None:
                        continue
                    ins = fn(eng)
                    if sig:
                        ins.then_inc(sems[sig[0]], sig[1])
            return body
        block.tensor(run("pe"))
        block.scalar(run("act"))
        block.vector(run("dve"))
        block.gpsimd(run("pool"))
        block.sync(run("sp"))


def build(seq, nlast=L):
    fused = seq == 'fused'
    if fused:
        seq = ['setup'] + [t for l in range(nlast) for t in (f'A{l}', f'X{l}', f'B{l}', f'C{l}')]
    need = {}
    for t in seq:
        if t[0] == 'A':
            need.setdefault(int(t[1:]), set()).add('WA')
        if t[0] == 'C':
            need.setdefault(int(t[1:]), set()).update(['WC2', 'WM', 'WO', 'W13', 'W2'])
    has_setup = 'setup' in seq
    is_final = seq[-1] == f'C{nlast - 1}'
    nc = bass.Bass("TRN2", target_bir_lowering=False)
    es = ExitStack()
    with es:
        P = Prog(nc, es)

        def din(name, shape, dt=F32):
            return nc.dram_tensor(name, list(shape), dt, kind="ExternalInput").ap()

        if has_setup:
            xin = din("xin", [128, 8, NT])
            cT_d = din("cT", [128, 8, 2])
            wmod_d = din("wmod", [L, D, 1536])
            bmod_d = din("bmodT", [128, L, 12])
            n12_d = din("n12T", [128, L, 2, 8])
        gain_d = din("gains", [128, L, 4])
        convw_d = din("convw", [128, L, 2, 3])
        sgn_d = din("sgn", [128, L, 256])
        bs_d = din("bsT", [128, L, 2, 128])
        wsT_d = din("wsT", [128, L, 4, 128])
        tab_d = din("tabs", [128, 2, NLAT])
        mask_d = din("masks", [128, 2, 4])
        wd = {k: din(k, [L, 128, WSPEC[k][1] * WSPEC[k][0] * 128]) for k in sorted(set().union(*need.values()))}
        xout = nc.dram_tensor("xout", [128, 8, NLAT], F32, kind="ExternalOutput").ap() if is_final else None
        wsc = {(l, k): nc.dram_tensor(f"wsc_{k}_{l}", [128, WSPEC[k][1] * WSPEC[k][0] * 128], BF16)
               for l in need for k in need[l]}
        sndk_d = nc.dram_tensor("sndk_d", [128, 2048], BF16)
        gatk_d = nc.dram_tensor("gatk_d", [512, 2048], BF16)
        sndv_d = nc.dram_tensor("sndv_d", [128, 3088], BF16)
        gatv_d = nc.dram_tensor("gatv_d", [512, 3088], BF16)

        def sb(name, shape, dt):
            return es.enter_context(nc.sbuf_tensor(name, list(shape), dt))

        X = sb("X", [128, 8, NT], F32)
        QA = sb("QA", [128, 4, NT], BF16)
        ARENA = sb("ARENA", [128, 21120], BF16)
        SND = sb("SND", [128, 5136], BF16)
        WR = sb("WR", [128, NSLOT, SLOT], BF16)
        HT0 = sb("HT", [128, 8, 512], BF16)
        HT1 = sb("HT1", [128, 8, 512], BF16)
        HTS = [HT0, HT1]
        cur = {"h": 0}
        RHTS = None
        NTS = 4
        TS = sb("TS", [128, NTS, 512], F32)
        SQ = sb("SQ", [128, 2, 512], BF16)
        RS = sb("RS", [128, 512], F32)
        PS_ = sb("PS", [128, 2, 2, 512], BF16)
        ZE = sb("ZE", [128, 2, 516], F32)
        ONES = sb("ONES", [128, 128], BF16)
        BONES = sb("BONES", [128, 128], BF16)
        cT = sb("cTs", [128, 8, 2], F32)
        scT = sb("scT", [128, 8, 2], F32)
        bmod = sb("bmod", [128, L, 12], F32)
        MODP = sb("MODP", [128, L, 12, 2], F32)
        n12 = sb("n12", [128, L, 2, 8], F32)
        gains = sb("gainss", [128, L, 4], F32)
        convw = sb("convws", [128, L, 2, 3], F32)
        sgn = sb("sgns", [128, 256], F32)
        bsT = sb("bsTs", [128, 2, 128], F32)
        wsT = sb("wsTs", [128, 4, 128], BF16)
        masks = sb("maskss", [128, 2, 4], F32)
        modT = sb("modT", [128, L, 48, 2], F32)
        SB12 = sb("SB12", [128, L, 2, 8, 2], F32)
        ZB = sb("ZB", [128, 5, 2, 2], F32)
        HALO = sb("HALO", [128, 4, 8], BF16)
        HSEL = sb("HSEL", [128, 2, 4], F32)
        HTMP = sb("HTMP", [128, 4, 4], F32)
        SS1 = sb("SS1", [128, 4], F32)
        PSB = es.enter_context(nc.psum_tensor("PSB", [128, 8, 512], F32))
        banks = [PSB[:, i, :] for i in range(8)]
        RB = [Res(f"bank{i}") for i in range(8)]

        Kall = ARENA[:, 0:8448]
        Vall = ARENA[:, 8448:21120].rearrange("p (t c) -> p t c", c=192)
        TAB = ARENA[:, 8448:8448 + 8192].bitcast(F32).rearrange("p (a n) -> p a n", a=2)
        STG = ARENA[:, 0:16384].bitcast(F32).rearrange("p (s k n) -> p s k n", s=2, k=8)
        ACTB = ARENA[:, 0:11264].rearrange("p (k n) -> p k n", n=512)
        MB = ARENA[:, 11264:15360].rearrange("p (k n) -> p k n", n=512)
        ABB = ARENA[:, 15360:16384].rearrange("p (k n) -> p k n", n=512)
        UGB = ARENA[:, 16384:17408].rearrange("p (k n) -> p k n", n=512)
        YCB = ARENA[:, 17408:18432].rearrange("p (k n) -> p k n", n=512)
        YAB = ARENA[:, 18432:19456].rearrange("p (k n) -> p k n", n=512)
        SVNZ = ARENA[:, 19456:20480].rearrange("p (t g n) -> p t g n", t=2, g=4)

        R = {}

        def res(name):
            if name not in R:
                R[name] = Res(name)
            return R[name]
        RX = [res(f"x{b}") for b in range(5)]
        RQA = [[res(f"qa{c}_{b}") for b in range(5)] for c in range(4)]
        RHTS = [[res(f"ht{h}_{kc}") for kc in range(8)] for h in range(2)]
        RT = [res(f"ts{i}") for i in range(NTS)]
        RSQ = [res("sq0"), res("sq1")]
        RPS = [res(f"ps{i}") for i in range(2)]
        RSLOT = [res(f"slot{i}") for i in range(NSLOT)]
        st = {"bank": 0, "ts": 0, "sq": 0, "slot": 0, "qz": 0, "pools": list(range(8))}

        def nbank():
            pool = st["pools"]
            b = pool[st["bank"] % len(pool)]
            st["bank"] += 1
            return b

        def nts():
            i = st["ts"] % NTS
            st["ts"] += 1
            return i

        def act(out, in_, func, rd, wr, **kw):
            return P.emit("act", lambda e: e.activation(out=out, in_=in_, func=func, **kw), rd, wr)

        def tt(out, in0, in1, op, rd, wr):
            return P.emit("dve", lambda e: e.tensor_tensor(out=out, in0=in0, in1=in1, op=op), rd, wr)

        def ptt(out, in0, in1, op, rd, wr):
            return P.emit("pool", lambda e: e.tensor_tensor(out=out, in0=in0, in1=in1, op=op), rd, wr)

        def stt(out, in0, scalar, in1, op0, op1, rd, wr):
            return P.emit("dve", lambda e: e.scalar_tensor_tensor(out=out, in0=in0, scalar=scalar, in1=in1, op0=op0, op1=op1), rd, wr)

        def ts_(out, in0, s1, s2, op0, op1, rd, wr):
            if op1 is None:
                return P.emit("dve", lambda e: e.tensor_scalar(out=out, in0=in0, scalar1=s1, scalar2=None, op0=op0), rd, wr)
            return P.emit("dve", lambda e: e.tensor_scalar(out=out, in0=in0, scalar1=s1, scalar2=s2, op0=op0, op1=op1), rd, wr)

        def rcp(out, in_, rd, wr):
            return P.emit("dve", lambda e: e.reciprocal(out=out, in_=in_), rd, wr)

        def cpy(out, in_, rd, wr):
            return P.emit("dve", lambda e: e.tensor_copy(out=out, in_=in_), rd, wr)

        def mset(ap, val, wr):
            return P.emit("dve", lambda e: e.memset(ap, val), (), wr)

        def mmg(mms, rd, wr):
            n = len(mms)
            for i, (o, a, b) in enumerate(mms):
                last = i == n - 1
                fn = (lambda o=o, a=a, b=b, i=i, last=last: (lambda e: e.matmul(o, lhsT=a, rhs=b, start=(i == 0), stop=last)))()
                if last:
                    P.emit("pe", fn, rd, wr)
                else:
                    P.emit("pe", fn, rd if i == 0 else (), (), sig=False, also=wr if i == 0 else ())

        def dma(q, out, in_, rd, wr, sem):
            P.sem(sem)
            return P.emit(q, lambda e: e.dma_start(out=out, in_=in_), rd, wr, sig=sem, inc=16)

        RSC = {k: res(f"wsc_{k[1]}_{k[0]}") for k in wsc}

        def convert_layer(l, only=None):
            for k in [k_ for k_ in ("WA", "WC2", "WM", "WO", "W13", "W2") if k_ in need.get(l, ()) and (only is None or k_ in only)]:
                kc, oc = WSPEC[k]
                tot = oc * kc * 128
                step = 8192 * 2
                for a in range(0, tot, step):
                    b = min(tot, a + step)
                    P.sem(f"cv{l}_{k}")
                    P.emit("pool", (lambda a=a, b=b, k=k: (lambda e: e.dma_start(out=wsc[(l, k)].ap()[:, a:b].rearrange("p (n c) -> p n c", c=1024), in_=wd[k][l, :, a:b].rearrange("p (n c) -> p n c", c=1024))))(),
                           (), [RSC[(l, k)]], sig=f"cv{l}_{k}", inc=16)

        def wload(l, k, oc0, n):
            kc, oc = WSPEC[k]
            s = st["slot"] % NSLOT
            st["slot"] += 1
            sz = n * kc * 128
            assert sz <= SLOT
            src = wsc[(l, k)].ap()[:, oc0 * kc * 128: oc0 * kc * 128 + sz]
            dma("sp", WR[:, s, 0:sz], src, [RSC[(l, k)]], [RSLOT[s]], f"ws{s}")
            return WR[:, s, 0:sz].rearrange("p (u k j) -> p u k j", u=n, k=kc), RSLOT[s]

        class Stream:
            def __init__(self, l):
                self.l = l
                self.items = []

            def add(self, k, oc0, n, fn):
                self.items.append((k, oc0, n, fn))

            def run(self):
                pend = None
                for i, (k, oc0, n, fn) in enumerate(self.items):
                    if pend is None:
                        pend = wload(self.l, k, oc0, n)
                    cur = pend
                    pend = None
                    if i + 1 < len(self.items) and NSLOT > 1:
                        k2, o2, n2, _ = self.items[i + 1]
                        pend = wload(self.l, k2, o2, n2)
                    fn(cur[0], cur[1])

        for (name, t, d) in [("gains", gains, gain_d), ("convw", convw, convw_d), ("masks", masks, mask_d)]:
            dma("sp", t[:], d, (), [res(name)], "ld_" + name)
        mset(ONES[:], 1.0, [res("ones")])
        mset(BONES[:], 0.0, [res("bones")])
        mset(BONES[0:64, 0:64], 1.0, [res("bones")])
        mset(BONES[64:128, 64:128], 1.0, [res("bones")])
        mset(SND[:, 2048:5136], 1.0, [res("snd_v")])
        STATE = [("X", X), ("QA", QA), ("ARENA", ARENA), ("ZB", ZB), ("modT", modT), ("SB12", SB12), ("HALO", HALO)]
        if has_setup:
            dma("sp", cT[:], cT_d, (), [res("cT")], "ld_cT")
            for (name, t, d) in [("bmod", bmod, bmod_d), ("n12", n12, n12_d)]:
                dma("sp", t[:], d, (), [res(name)], "ld_" + name)
            for b, (s0, n) in enumerate(BLOCKS):
                dma("sp", X[:, :, s0:s0 + n], xin[:, :, s0:s0 + n], (), [RX[b]], f"ldx{b}")
            mset(ZB[:], 0.0, [res("zb")])
        else:
            for nm, t in STATE:
                d = din("st_" + nm, list(t.shape), t.dtype)
                rs = {"X": RX, "QA": [r_ for rr in RQA for r_ in rr], "ARENA": [res("klat"), res("vlat"), res("kctx"), res("vctx")],
                      "ZB": [res("zb")], "modT": [res("modT")], "SB12": [res("sb12")], "HALO": [res("halo")]}[nm]
                dma("sp", t[:], d, (), rs, "lds_" + nm)
        for l in sorted(need):
            if not fused:
                convert_layer(l)
            elif l == 0:
                convert_layer(0, only=("WA",))

        if has_setup:
            act(scT[:], cT[:], AF.Silu, [res("cT")], [res("scT")])
            RSTG = [res("stg0"), res("stg1")]
            sndm_d = nc.dram_tensor("sndm_d", [128, L * 24], F32)
            gatm_d = nc.dram_tensor("gatm_d", [512, L * 24], F32)
            bk = nbank()
            for l in range(nlast):
                for j in range(3):
                    s = (l * 3 + j) % 2
                    dma("sp", STG[:, s], wmod_d[l, :, j * 512:(j + 1) * 512].rearrange("(k p) n -> p k n", p=128),
                        (), [RSTG[s]], f"stg{s}")
                    for oc in range(4):
                        col = l * 24 + (j * 4 + oc) * 2
                        mmg([(banks[bk][:, col:col + 2], STG[:, s, kc, oc * 128:(oc + 1) * 128], scT[:, kc, :]) for kc in range(8)],
                            [RSTG[s], res("scT")], [RB[bk]])
                for j in range(2):
                    tt(MODP[:, l, :, j], banks[bk][:, l * 24 + j:l * 24 + 24:2], bmod[:, l, :], ALU.add, [RB[bk], res("bmod")], [res("modp")])
            if nlast < L:
                mset(MODP[:, nlast:L], 0.0, [res("modp")])
            for nm in ("sxm", "ccm", "lgm"):
                P.sem(nm)
            P.emit("pool", lambda e: e.dma_start(out=sndm_d.ap(), in_=MODP[:].rearrange("p l o j -> p (l o j)")), [res("modp")], [res("sndm_d")], sig="sxm", inc=16)
            P.emit("pool", lambda e: e.collective_compute("AllGather", ALU.bypass, replica_groups=[[0, 1, 2, 3], [4, 5, 6, 7]],
                                                          ins=[sndm_d.ap().opt()], outs=[gatm_d.ap().opt()]),
                   [res("sndm_d")], [res("gatm_d")], sig="ccm", inc=1)
            for r in range(4):
                P.emit("pool", (lambda r=r: (lambda e: e.dma_start(out=modT[:, :, r * 12:(r + 1) * 12, :].rearrange("p l o j -> p l (o j)"),
                                                                   in_=gatm_d.ap()[r * 128:(r + 1) * 128, :].rearrange("p (l x) -> p l x", l=L))))(),
                       [res("gatm_d")], [res("modT")], sig="lgm", inc=16)
            for l in range(nlast):
                for nrm in range(2):
                    for j in range(2):
                        sc = modT[:, l, 8 + 24 * nrm:16 + 24 * nrm, j]
                        stt(SB12[:, l, nrm, :, j], sc, 1.0, n12[:, l, nrm, :], ALU.add, ALU.mult,
                            [res("modT"), res("n12")], [res("sb12")])
            P.barrier()

        def modv(l, which, kc, j):
            base = {"sh1": 0, "gt1": 16, "sh2": 24, "gt2": 40}[which]
            return modT[:, l, base + kc, j:j + 1]

        def norm_block(l, b, nrm, hsel=0):
            s0, n = BLOCKS[b]
            j = 1 if b == 4 else 0
            bk = nbank()
            mms = []
            for kc in range(8):
                q = st["sq"] % 2
                st["sq"] += 1
                act(SQ[:, q, 0:n], X[:, kc, s0:s0 + n], AF.Square, [RX[b]], [RSQ[q]])
                P.emit("pe", (lambda q=q, kc=kc: (lambda e: e.matmul(banks[bk][:, 0:n], lhsT=ONES[:], rhs=SQ[:, q, 0:n], start=(kc == 0), stop=(kc == 7))))(),
                       [RSQ[q], res("ones")], [RB[bk]] if kc == 7 else (), sig=None, also=[RB[bk]] if kc == 0 else ())
            t1 = nts()
            act(TS[:, t1, 0:n], banks[bk][:, 0:n], AF.Sqrt, [RB[bk]], [RT[t1]], scale=1.0 / D, bias=EPS)
            rcp(RS[:, 0:n], TS[:, t1, 0:n], [RT[t1]], [res("rs")])
            for kc in range(8):
                t3 = nts()
                if kc % 2 == 0:
                    tt(TS[:, t3, 0:n], X[:, kc, s0:s0 + n], RS[:, 0:n], ALU.mult, [RX[b], res("rs")], [RT[t3]])
                else:
                    ptt(TS[:, t3, 0:n], X[:, kc, s0:s0 + n], RS[:, 0:n], ALU.mult, [RX[b], res("rs")], [RT[t3]])
                act(HTS[hsel][:, kc, 0:n], TS[:, t3, 0:n], AF.Identity, [RT[t3], res("sb12"), res("modT")], [RHTS[hsel][kc]],
                    scale=SB12[:, l, nrm, kc, j:j + 1], bias=modv(l, "sh1" if nrm == 0 else "sh2", kc, j))

        class _Cur:
            def __getitem__(self, key):
                return HTS[cur["h"]][key]
        HT = _Cur()

        class _CurR(list):
            def __radd__(self, other):
                return other + RHTS[cur["h"]]
        RHT = _CurR()

        def proj_fm(wv, wr_, u, n, cols=None):
            bk = nbank()
            if cols is None:
                mmg([(banks[bk][:, 0:n], wv[:, u, kc, :], HT[:, kc, 0:n]) for kc in range(8)], [wr_] + RHT, [RB[bk]])
            else:
                mmg([(banks[bk][:, cols[0]:cols[0] + 2], wv[:, u, kc, :], HT[:, kc, 0:n:n - 1]) for kc in range(8)], [wr_] + RHT, [RB[bk]])
            return bk

        def qk_finish(l, b, bq, bqr, gi, out_ap, out_res):
            s0, n = BLOCKS[b]
            q = st["sq"] % 2
            st["sq"] += 1
            act(SQ[:, q, 0:n], banks[bq][:, 0:n], AF.Square, [RB[bq]], [RSQ[q]])
            bs = nbank()
            mmg([(banks[bs][:, 0:n], BONES[:], SQ[:, q, 0:n])], [RSQ[q], res("bones")], [RB[bs]])
            t1 = nts()
            act(TS[:, t1, 0:n], banks[bs][:, 0:n], AF.Sqrt, [RB[bs]], [RT[t1]], scale=1.0 / 64, bias=EPS)
            t2 = nts()
            rcp(TS[:, t2, 0:n], TS[:, t1, 0:n], [RT[t1]], [RT[t2]])
            g = gains[:, l, gi:gi + 1]
            if bqr is None:
                stt(out_ap, banks[bq][:, 0:n], g, TS[:, t2, 0:n], ALU.mult, ALU.mult, [RB[bq], RT[t2], res("gains")], [out_res])
                return
            gp = gains[:, l, gi + 1:gi + 2]
            t3 = nts()
            stt(TS[:, t3, 0:n], banks[bq][:, 0:n], g, TAB[:, 0, s0:s0 + n], ALU.mult, ALU.mult, [RB[bq], res("vlat"), res("gains")], [RT[t3]])
            t4 = nts()
            stt(TS[:, t4, 0:n], banks[bqr][:, 0:n], gp, TAB[:, 1, s0:s0 + n], ALU.mult, ALU.mult, [RB[bqr], res("vlat"), res("gains")], [RT[t4]])
            ptt(TS[:, t3, 0:n], TS[:, t3, 0:n], TS[:, t4, 0:n], ALU.add, [RT[t3], RT[t4]], [RT[t3]])
            ptt(out_ap, TS[:, t3, 0:n], TS[:, t2, 0:n], ALU.mult, [RT[t3], RT[t2]], [out_res])

        def phase_A(l):
            last = l == nlast - 1
            cur["h"] = 0
            dma("sp", TAB[:], tab_d, (), [res("vlat")], "ldtab")
            for b, (s0, n) in enumerate(BLOCKS):
                ctx = b == 4
                norm_block(l, b, 0)
                v0, r0 = wload(l, "WA", 0, 5)
                v1, r1 = wload(l, "WA", 5, 5)

                def unit(u):
                    return (v0, r0, u) if u < 5 else ((v1, r1, u - 5) if u < 10 else None)
                if not (ctx and last):
                    for c in range(4):
                        bq = proj_fm(v0, r0, c, n)
                        if ctx:
                            qk_finish(l, b, bq, None, 0, QA[:, c, s0:s0 + n], RQA[c][b])
                        else:
                            ur = 4 + c
                            vv, rr, uu = unit(ur)
                            bqr = proj_fm(vv, rr, uu, n)
                            qk_finish(l, b, bq, bqr, 0, QA[:, c, s0:s0 + n], RQA[c][b])
                bq = proj_fm(v1, r1, 3, n)
                if ctx:
                    qk_finish(l, b, bq, None, 2, Kall[:, 8192:8448], res("kctx"))
                else:
                    bqr = proj_fm(v1, r1, 4, n)
                    qk_finish(l, b, bq, bqr, 2, SND[:, s0:s0 + n], res("snd_k"))
                v2, r2 = wload(l, "WA", 10, 5)
                bk = nbank()
                nt_ = n // 128
                for t in range(nt_):
                    mmg([(banks[bk][:, t * 128:(t + 1) * 128], HT[:, kc, t * 128:(t + 1) * 128], v2[:, 0, kc, :]) for kc in range(8)],
                        [r2] + RHT, [RB[bk]])
                pv = banks[bk][:, 0:nt_ * 128].rearrange("p (t c) -> p t c", c=128)
                if ctx:
                    dst = Vall[:, 64:66, :]
                    rs = res("vctx")
                else:
                    dst = SND[:, 2048:5120].rearrange("p (t c) -> p t c", c=192)[:, (s0 // 128):(s0 // 128) + nt_, :]
                    rs = res("snd_v")
                act(dst[:, :, 0:64], pv[:, :, 0:64], AF.Copy, [RB[bk]], [rs])
                act(dst[:, :, 128:192], pv[:, :, 64:128], AF.Copy, [RB[bk]], [rs])
                if ctx:
                    P.emit("dve", lambda e: e.memset(Vall[:, 64:66, 64:128], 1.0), (), [res("vctx")])
                if not ctx:
                    bk = nbank()
                    for i, u in enumerate([1, 2, 3, 4]):
                        mmg([(banks[bk][:, 2 * i:2 * i + 2], v2[:, u, kc, :], HT[:, kc, 0:n:n - 1]) for kc in range(8)], [r2] + RHT, [RB[bk]])
                    t1 = nts()
                    act(TS[:, t1, 0:4], banks[bk][:, 0:4], AF.Copy, [RB[bk]], [RT[t1]])
                    tt(ZB[:, b, :, :], TS[:, t1, 0:4].rearrange("p (c f) -> p c f", f=2),
                       banks[bk][:, 4:8].rearrange("p (c f) -> p c f", f=2), ALU.mult, [RB[bk], RT[t1]], [res("zb")])
                    if b == 0:
                        cpy(SND[:, 5120:5124:2], ZB[:, 0, :, 0], [res("zb")], [res("snd_h")])
                    if b == 3:
                        cpy(SND[:, 5121:5125:2], ZB[:, 3, :, 1], [res("zb")], [res("snd_h")])

        def exchange(l):
            for nm in ("sxk", "sxv", "cck", "ccv", "lg0", "lg1", "lg2"):
                P.sem(nm)
            rg = [[0, 1, 2, 3], [4, 5, 6, 7]]
            P.emit("pool", lambda e: e.dma_start(out=sndk_d.ap(), in_=SND[:, 0:2048]), [res("snd_k")], [res("sndk_d")], sig="sxk", inc=16)
            P.emit("pool", lambda e: e.collective_compute("AllGather", ALU.bypass, replica_groups=rg, ins=[sndk_d.ap().opt()], outs=[gatk_d.ap().opt()]),
                   [res("sndk_d")], [res("gatk_d")], sig="cck", inc=1)
            P.emit("pool", lambda e: e.dma_start(out=sndv_d.ap(), in_=SND[:, 2048:5136]), [res("snd_v"), res("snd_h")], [res("sndv_d")], sig="sxv", inc=16)
            P.emit("pool", lambda e: e.collective_compute("AllGather", ALU.bypass, replica_groups=rg, ins=[sndv_d.ap().opt()], outs=[gatv_d.ap().opt()]),
                   [res("sndv_d")], [res("gatv_d")], sig="ccv", inc=1)
            gk = gatk_d.ap().rearrange("(r p) c -> p r c", p=128)
            gv = gatv_d.ap().rearrange("(r p) c -> p r c", p=128)
            P.emit("pool", lambda e: e.dma_start(out=Kall[:, 0:8192].rearrange("p (r c) -> p r c", r=4), in_=gk),
                   [res("gatk_d")], [res("klat")], sig="lg0", inc=16)
            P.emit("pool", lambda e: e.dma_start(out=ARENA[:, 8448:8448 + 12288].rearrange("p (r c) -> p r c", r=4), in_=gv[:, :, 0:3072]),
                   [res("gatv_d")], [res("vlat")], sig="lg1", inc=16)
            P.emit("pool", lambda e: e.dma_start(out=HALO[:], in_=gv[:, :, 3072:3080]),
                   [res("gatv_d")], [res("halo")], sig="lg2", inc=16)
            if fused and l == 0:
                convert_layer(0, only=("WC2", "WM", "WO", "W13", "W2"))
            if fused and l + 1 < nlast:
                convert_layer(l + 1)

        def halo_select():
            for side in range(2):
                tt(HTMP[:], HALO[:, :, 0:4], masks[:, side, :].unsqueeze(2).to_broadcast([128, 4, 4]), ALU.mult,
                   [res("halo"), res("masks")], [res("htmp")])
                P.emit("dve", (lambda side=side: (lambda e: e.tensor_reduce(out=HSEL[:, side, :], in_=HTMP[:].rearrange("p r x -> p x r"),
                                                                            axis=mybir.AxisListType.X, op=ALU.add)))(),
                       [res("htmp")], [res("hsel")])

        def phase_B(l):
            last = l == nlast - 1
            for b, (s0, n) in [(4, BLOCKS[4])] + list(enumerate(BLOCKS[:4])):
                ctx = b == 4
                if ctx and last:
                    continue
                kts = [64, 65] if ctx else list(range(66))
                for c in range(4):
                    oa, ob = ((0, 1), (6, 7))[st["qz"] % 2]
                    st["qz"] += 1
                    sb_ = {}

                    def s_mm(kt, i):
                        ba = (2, 4)[i % 2]
                        bb = ba + 1
                        rk = res("kctx") if kt >= 64 else res("klat")
                        mmg([(banks[ba][:, 0:n], Kall[0:64, kt * 128:(kt + 1) * 128], QA[0:64, c, s0:s0 + n])], [rk, RQA[c][b]], [RB[ba]])
                        mmg([(banks[bb][:, 0:n], Kall[64:128, kt * 128:(kt + 1) * 128], QA[64:128, c, s0:s0 + n])], [rk, RQA[c][b]], [RB[bb]])
                        sb_[kt] = ba
                    s_mm(kts[0], 0)
                    for i, kt in enumerate(kts):
                        if i + 1 < len(kts):
                            s_mm(kts[i + 1], i + 1)
                        ba = sb_.pop(kt)
                        pp = i % 2
                        act(PS_[:, pp, :, 0:n], PSB[:, ba:ba + 2, 0:n], AF.Exp, [RB[ba], RB[ba + 1]], [RPS[pp]], scale=0.125)
                        rv = res("vctx") if kt >= 64 else res("vlat")
                        fst, lst = i == 0, i == len(kts) - 1
                        for (o, h, lo) in ((oa, 0, 0), (ob, 1, 64)):
                            P.emit("pe", (lambda o=o, h=h, pp=pp, lo=lo, kt=kt, fst=fst, lst=lst, n=n:
                                          (lambda e: e.matmul(banks[o][:, 0:n], lhsT=Vall[:, kt, lo:lo + 128], rhs=PS_[:, pp, h, 0:n], start=fst, stop=lst)))(),
                                   [rv, RPS[pp]], [RB[o]] if lst else (), sig=None, also=[RB[o]] if fst else ())
                    t1 = nts()
                    rcp(TS[:, t1, 0:n], banks[oa][:, 0:n], [RB[oa]], [RT[t1]])
                    tt(QA[0:64, c, s0:s0 + n], banks[oa][0:64, 0:n], TS[64:128, t1, 0:n], ALU.mult, [RB[oa], RT[t1]], [RQA[c][b]])
                    t2 = nts()
                    rcp(TS[:, t2, 0:n], banks[ob][:, 0:n], [RB[ob]], [RT[t2]])
                    tt(QA[64:128, c, s0:s0 + n], banks[ob][64:128, 0:n], TS[0:64, t2, 0:n], ALU.mult, [RB[ob], RT[t2]], [RQA[c][b]])

        def phase_C(l):
            last = l == nlast - 1
            P.sem("ldw")
            P.emit("pool", lambda e: e.dma_start(out=wsT[:], in_=wsT_d[:, l]), (), [res("wsT")], sig="ldw", inc=16)
            dma("sp", sgn[:], sgn_d[:, l, :], (), [res("sgn")], "ldp0")
            dma("sp", bsT[:], bs_d[:, l], (), [res("bsT")], "ldp1")
            def blockgen(b, hb):
                s0, n = BLOCKS[b]
                ctx = b == 4
                j = 1 if ctx else 0
                nt_ = n // 128
                norm_block(l, b, 0, hb)
                yield "N1"
                cur["h"] = hb
                v0, r0 = wload(l, "WC2", 0, 5)
                v1, r1 = wload(l, "WC2", 5, 5)

                def un(u):
                    return (v0, r0, u) if u < 5 else (v1, r1, u - 5)
                def do_ab(ch):
                    bk = proj_fm(*un(ch), n)
                    act(ABB[:, ch, 0:n], banks[bk][:, 0:n], AF.Copy, [RB[bk]], [res(f"ab{ch}")])

                def do_acax(ch):
                    bk = proj_fm(*un(2 + ch), n)
                    t1 = nts()
                    act(TS[:, t1, 0:n], banks[bk][:, 0:n], AF.Copy, [RB[bk]], [RT[t1]])
                    bk2 = proj_fm(*un(4 + ch), n)
                    rz = res(f"ze{ch}")
                    tt(ZE[:, ch, 1:n + 1], banks[bk2][:, 0:n], TS[:, t1, 0:n], ALU.mult, [RB[bk2], RT[t1]], [rz])
                    if ctx:
                        mset(ZE[:, ch, 0:1], 0.0, [rz])
                        mset(ZE[:, ch, n + 1:n + 2], 0.0, [rz])
                    else:
                        if b == 0:
                            cpy(ZE[:, ch, 0:1], HSEL[:, 0, 2 * ch + 1:2 * ch + 2], [res("hsel")], [rz])
                        else:
                            cpy(ZE[:, ch, 0:1], ZB[:, b - 1, ch, 1:2], [res("zb")], [rz])
                        if b == 3:
                            cpy(ZE[:, ch, n + 1:n + 2], HSEL[:, 1, 2 * ch:2 * ch + 1], [res("hsel")], [rz])
                        else:
                            cpy(ZE[:, ch, n + 1:n + 2], ZB[:, b + 1, ch, 0:1], [res("zb")], [rz])
                    t2 = nts()
                    ts_(TS[:, t2, 0:n], ZE[:, ch, 1:n + 1], convw[:, l, ch, 1:2], None, ALU.mult, None, [rz, res("convw")], [RT[t2]])
                    stt(TS[:, t2, 0:n], ZE[:, ch, 0:n], convw[:, l, ch, 0:1], TS[:, t2, 0:n], ALU.mult, ALU.add, [rz, RT[t2]], [RT[t2]])
                    stt(TS[:, t2, 0:n], ZE[:, ch, 2:n + 2], convw[:, l, ch, 2:3], TS[:, t2, 0:n], ALU.mult, ALU.add, [rz, RT[t2]], [RT[t2]])
                    tt(YAB[:, ch, 0:n], TS[:, t2, 0:n], ABB[:, ch, 0:n], ALU.mult, [RT[t2], res(f"ab{ch}")], [res(f"ya{ch}")])

                def do_u(ch):
                    bk = proj_fm(*un(6 + ch), n)
                    act(UGB[:, ch, 0:n], banks[bk][:, 0:n], AF.Gelu, [RB[bk]], [res(f"ug{ch}")])

                mset(SVNZ[:], 0.0, [res("svn0"), res("svn1")])
                mixb = [nbank(), nbank()]
                st["pools"] = [i for i in range(8) if i not in mixb]

                def do_sv(t):
                    bk = nbank()
                    for uu in range(2):
                        vv, rr, u_ = un(8 + uu)
                        mmg([(banks[bk][:, uu * 128:(uu + 1) * 128], HT[:, kc, t * 128:(t + 1) * 128], vv[:, u_, kc, :]) for kc in range(8)],
                            [rr] + RHT, [RB[bk]])
                    t1 = nts()
                    act(TS[:, t1, 0:256], banks[bk][:, 0:256], AF.Gelu, [RB[bk]], [RT[t1]])
                    t2 = nts()
                    act(TS[:, t2, 0:256], TS[:, t1, 0:256], AF.Square, [RT[t1]], [RT[t2], res("ss1")], accum_out=SS1[:, t:t + 1])
                    act(SS1[:, t:t + 1], SS1[:, t:t + 1], AF.Sqrt, [res("ss1")], [res("ss1")], scale=1.0 / 256, bias=EPS)
                    P.emit("dve", (lambda t=t: (lambda e: e.reciprocal(out=SS1[:, t:t + 1], in_=SS1[:, t:t + 1])))(), [res("ss1")], [res("ss1")])
                    tb = t % 2
                    rsv = res(f"svn{tb}")
                    for g in range(4):
                        gl = g % 2
                        stt(SVNZ[:, tb, g, gl * 64:(gl + 1) * 64], TS[:, t1, g * 64:(g + 1) * 64], SS1[:, t:t + 1], sgn[:, g * 64:(g + 1) * 64],
                            ALU.mult, ALU.mult, [RT[t1], res("ss1"), res("sgn")], [rsv])

                def do_mix(t):
                    tb = t % 2
                    for ch in range(2):
                        mmg([(banks[mixb[ch]][:, t * 128:(t + 1) * 128], SVNZ[:, tb, 2 * ch + gl, :], wsT[:, 2 * ch + gl, :]) for gl in range(2)],
                            [res(f"svn{tb}"), res("wsT")], [RB[mixb[ch]]])

                do_sv(0); do_sv(1); do_ab(0); do_ab(1); do_mix(0); do_mix(1)
                if nt_ == 4:
                    do_sv(2); do_sv(3); do_acax(0); do_mix(2); do_acax(1); do_mix(3)
                else:
                    do_acax(0); do_acax(1)
                do_u(0); do_u(1)
                st["pools"] = list(range(8))
                for ch in range(2):
                    bk = mixb[ch]
                    t1 = nts()
                    for t in range(nt_):
                        tt(TS[:, t1, t * 128:(t + 1) * 128], banks[bk][:, t * 128:(t + 1) * 128], bsT[:, ch, :], ALU.add, [RB[bk], res("bsT")], [RT[t1]])
                    tt(YCB[:, ch, 0:n], TS[:, t1, 0:n], UGB[:, ch, 0:n], ALU.mult, [RT[t1], res(f"ug{ch}")], [res(f"yc{ch}")])
                yield "C2"
                cur["h"] = hb
                S = Stream(l)
                for oc in range(8):
                    def fm(wv, wr_, oc=oc):
                        srcs = [(0, 2, YAB, [res("ya0"), res("ya1")]), (2, 4, QA, None), (6, 2, YCB, [res("yc0"), res("yc1")])]
                        tacc = None
                        for bi, (k0, nk, src, rs) in enumerate(srcs):
                            bg = nbank()
                            mmg([(banks[bg][:, 0:n], wv[:, 0, 8 + 8 * bi + kc, :], HT[:, kc, 0:n]) for kc in range(8)], [wr_] + RHT, [RB[bg]])
                            by = nbank()
                            if src is QA:
                                mmg([(banks[by][:, 0:n], wv[:, 0, k0 + kc, :], QA[:, kc, s0:s0 + n]) for kc in range(nk)],
                                    [wr_] + [RQA[kc][b] for kc in range(4)], [RB[by]])
                            else:
                                mmg([(banks[by][:, 0:n], wv[:, 0, k0 + kc, :], src[:, kc, 0:n]) for kc in range(nk)], [wr_] + rs, [RB[by]])
                            t1 = nts()
                            act(TS[:, t1, 0:n], banks[bg][:, 0:n], AF.Sigmoid, [RB[bg]], [RT[t1]])
                            tt(TS[:, t1, 0:n], banks[by][:, 0:n], TS[:, t1, 0:n], ALU.mult, [RB[by], RT[t1]], [RT[t1]])
                            if tacc is None:
                                tacc = t1
                            elif bi == 1:
                                tt(TS[:, tacc, 0:n], TS[:, tacc, 0:n], TS[:, t1, 0:n], ALU.add, [RT[tacc], RT[t1]], [RT[tacc]])
                            else:
                                tt(MB[:, oc, 0:n], TS[:, tacc, 0:n], TS[:, t1, 0:n], ALU.add, [RT[tacc], RT[t1]], [res(f"m{oc}")])
                    S.add("WM", oc, 1, fm)
                RM = [res(f"m{k}") for k in range(8)]

                def make_fo(o0, cnt):
                    def fo(wv, wr_):
                        for u in range(cnt):
                            oc = o0 + u
                            bk = nbank()
                            mmg([(banks[bk][:, 0:n], wv[:, u, kc, :], MB[:, kc, 0:n]) for kc in range(8)], [wr_] + RM, [RB[bk]])
                            stt(X[:, oc, s0:s0 + n], banks[bk][:, 0:n], modv(l, "gt1", oc, j), X[:, oc, s0:s0 + n], ALU.mult, ALU.add,
                                [RB[bk], res("modT"), RX[b]], [RX[b]])
                    return fo
                S.add("WO", 0, 4, make_fo(0, 4))
                S.add("WO", 4, 4, make_fo(4, 4))
                S.run()
                yield "MW"
                norm_block(l, b, 1, hb)
                yield "N2"
                cur["h"] = hb
                S = Stream(l)

                def make_f13(j0, cnt):
                    def f13(wv, wr_):
                        for u in range(cnt):
                            jj = j0 + u
                            b1 = nbank()
                            mmg([(banks[b1][:, 0:n], wv[:, 2 * u, kc, :], HT[:, kc, 0:n]) for kc in range(8)], [wr_] + RHT, [RB[b1]])
                            b3 = nbank()
                            mmg([(banks[b3][:, 0:n], wv[:, 2 * u + 1, kc, :], HT[:, kc, 0:n]) for kc in range(8)], [wr_] + RHT, [RB[b3]])
                            t1 = nts()
                            act(TS[:, t1, 0:n], banks[b1][:, 0:n], AF.Silu, [RB[b1]], [RT[t1]])
                            tt(ACTB[:, jj, 0:n], banks[b3][:, 0:n], TS[:, t1, 0:n], ALU.mult, [RB[b3], RT[t1]], [res(f"act{jj}")])
                    return f13
                for j0 in range(0, 22, 2):
                    S.add("W13", 2 * j0, 4, make_f13(j0, 2))
                RACT = [res(f"act{k}") for k in range(22)]

                def make_f2(o0):
                    def f2(wv, wr_):
                        for u in range(2):
                            oc = o0 + u
                            bk = nbank()
                            mmg([(banks[bk][:, 0:n], wv[:, u, kc, :], ACTB[:, kc, 0:n]) for kc in range(22)], [wr_] + RACT, [RB[bk]])
                            stt(X[:, oc, s0:s0 + n], banks[bk][:, 0:n], modv(l, "gt2", oc, j), X[:, oc, s0:s0 + n], ALU.mult, ALU.add,
                                [RB[bk], res("modT"), RX[b]], [RX[b]])
                    return f2
                for o0 in range(0, 8, 2):
                    S.add("W2", o0, 2, make_f2(o0))
                S.run()
                if last:
                    P.sem("st")
                    dma("sp", xout[:, :, s0:s0 + n], X[:, :, s0:s0 + n], [RX[b]], [res("xout")], "st")
                yield "F"

            order = [b for b in range(5) if not (b == 4 and last)]
            gens = [blockgen(b, i % 2) for i, b in enumerate(order)]
            assert next(gens[0]) == "N1" and next(gens[0]) == "C2"
            for i in range(len(order)):
                assert next(gens[i]) == "MW"
                if i + 1 < len(order):
                    assert next(gens[i + 1]) == "N1"
                assert next(gens[i]) == "N2"
                if i + 1 < len(order):
                    assert next(gens[i + 1]) == "C2"
                assert next(gens[i]) == "F"

        for t in seq:
            if t == 'setup':
                continue
            l = int(t[1:])
            if t[0] == 'A':
                phase_A(l)
            elif t[0] == 'X':
                exchange(l)
            elif t[0] == 'B':
                halo_select()
                phase_B(l)
                P.barrier()
            elif t[0] == 'C':
                phase_C(l)
                P.barrier()
        if not is_final:
            P.barrier()
            for nm, t in STATE[:-1] + [("SND", SND)]:
                o = nc.dram_tensor("o_" + nm, list(t.shape), t.dtype, kind="ExternalOutput").ap()
                dma("sp", o, t[:], [], [res("dbgout")], "st")
        P.sem("st")
        P.emit("sp", None, extra_deps=[("st", P.cnt["st"])], sig=False)
        block = es.enter_context(nc.Block())
        P.replay(block)
    return nc


HD = 64


def _partner():
    d = np.arange(64)
    half = (d % 32) // 16
    return np.where(half == 0, d + 16, d - 16)


def _unit_layout(w):
    K, N = w.shape
    kc, oc = K // 128, N // 128
    return np.ascontiguousarray(w.reshape(kc, 128, oc, 128).transpose(1, 2, 0, 3)).reshape(128, oc * kc * 128)


def _prep_shared(inp):
    f = np.float32
    w_in = np.asarray(inp["w_in"], f)
    pt = _partner()
    OFF_Q, OFF_K, OFF_V, OFF_U, OFF_SV, OFF_G = 768, 1280, 1408, 1536, 1792, 2048
    qcols = np.concatenate([np.concatenate([OFF_Q + c * 64 + np.arange(64), OFF_Q + (4 + c) * 64 + np.arange(64)]) for c in range(4)])
    qrcols = np.concatenate([np.concatenate([OFF_Q + c * 64 + pt, OFF_Q + (4 + c) * 64 + pt]) for c in range(4)])
    kcols = OFF_K + np.arange(128)
    krcols = np.concatenate([OFF_K + pt, OFF_K + 64 + pt])
    vcols = OFF_V + np.arange(128)
    ab, ac, ax = np.arange(0, 256), np.arange(256, 512), np.arange(512, 768)
    u, sv = OFF_U + np.arange(256), OFF_SV + np.arange(256)
    colsA = np.concatenate([qcols, qrcols, kcols, krcols, vcols, ac, ax])
    colsC2 = np.concatenate([ab, ac, ax, u, sv])
    brow = np.concatenate([np.concatenate([c * 64 + np.arange(64), (4 + c) * 64 + np.arange(64)]) for c in range(4)])
    out = {k: [] for k in WSPEC}
    for l in range(L):
        out["WA"].append(_unit_layout(w_in[l][:, colsA]))
        out["WC2"].append(_unit_layout(w_in[l][:, colsC2]))
        wm = np.concatenate([np.asarray(inp["w_a"][l], f), np.asarray(inp["w_b"][l], f)[brow], np.asarray(inp["w_c"][l], f),
                             w_in[l][:, OFF_G:OFF_G + 1024], w_in[l][:, OFF_G + 1024:OFF_G + 2048], w_in[l][:, OFF_G + 2048:OFF_G + 3072]], 0)
        out["WM"].append(_unit_layout(wm))
        out["WO"].append(_unit_layout(np.asarray(inp["w_o"][l], f)))
        w1, w3 = np.asarray(inp["w_ff1"][l], f), np.asarray(inp["w_ff3"][l], f)
        w13 = np.stack([w1.reshape(D, 22, 128), w3.reshape(D, 22, 128)], 2).reshape(D, 44 * 128)
        out["W13"].append(_unit_layout(w13))
        out["W2"].append(_unit_layout(np.asarray(inp["w_ff2"][l], f)))
    sh = {k: np.stack(v) for k, v in out.items()}
    p = np.arange(128)
    sh["_wmod"] = np.asarray(inp["w_mod"], f)
    sh["_bmodT"] = np.ascontiguousarray(np.asarray(inp["b_mod"], f).reshape(L, 48, 128).transpose(2, 0, 1))
    n12 = np.stack([np.asarray(inp["norm1"], f), np.asarray(inp["norm2"], f)], 1)
    sh["n12T"] = np.ascontiguousarray(n12.reshape(L, 2, 8, 128).transpose(3, 0, 1, 2))
    qg, kg = np.asarray(inp["q_gain"], f), np.asarray(inp["k_gain"], f)
    d = p % 64
    sh["gains"] = np.ascontiguousarray(np.stack([qg[:, d], qg[:, pt[d]], kg[:, d], kg[:, pt[d]]], -1).transpose(1, 0, 2))
    cw = np.asarray(inp["conv_w"], f)
    sh["convw"] = np.ascontiguousarray(cw.reshape(L, 3, 2, 128).transpose(3, 0, 2, 1))
    sh["sgn"] = np.ascontiguousarray(np.broadcast_to(np.asarray(inp["sg_norm"], f)[None], (128, L, 256)))
    bs = np.asarray(inp["b_s"], f)
    sh["bsT"] = np.ascontiguousarray(bs.reshape(L, 2, 2, 128)[:, :, p // 64, :].transpose(2, 0, 1, 3))
    ws = np.asarray(inp["w_s"], f)
    sh["wsT"] = np.ascontiguousarray(ws.transpose(3, 0, 1, 2))
    return sh


def _rope_tabs(j):
    f = np.float32
    n = j * NLAT + np.arange(NLAT)
    row = (n // 64).astype(f)
    col = (n % 64).astype(f)
    inv = (np.float32(10000.0) ** (-np.arange(0, 32, 2, dtype=f) / np.float32(32))).astype(f)
    d = np.arange(128) % 64
    axis, half, fr = d // 32, (d % 32) // 16, d % 16
    pos = np.where(axis[:, None] == 0, row[None, :], col[None, :]).astype(f)
    ang = (pos * inv[fr][:, None]).astype(f)
    cos = np.cos(ang).astype(f)
    sin = np.sin(ang).astype(f)
    sin = np.where(half[:, None] == 0, -sin, sin).astype(f)
    return np.ascontiguousarray(np.stack([cos, sin], 1))


def make_in_maps(inp):
    f = np.float32
    sh = _prep_shared(inp)
    x, ctx, c, c_ctx = (np.asarray(inp[k], f) for k in ("x", "ctx", "c", "c_ctx"))
    maps = []
    for r in range(8):
        b, j = r // 4, r % 4
        xt = np.concatenate([x[b, j * NLAT:(j + 1) * NLAT], ctx[b]], 0)
        m = {k: v for k, v in sh.items() if not k.startswith("_")}
        m["wmod"] = np.ascontiguousarray(sh["_wmod"][:, :, 1536 * j:1536 * (j + 1)])
        m["bmodT"] = np.ascontiguousarray(sh["_bmodT"][:, :, 12 * j:12 * (j + 1)])
        m["xin"] = np.ascontiguousarray(xt.reshape(NT, 8, 128).transpose(2, 1, 0))
        m["cT"] = np.ascontiguousarray(np.stack([c[b], c_ctx], -1).reshape(8, 128, 2).transpose(1, 0, 2))
        m["tabs"] = _rope_tabs(j)
        mk = np.zeros((128, 2, 4), f)
        if j > 0:
            mk[:, 0, j - 1] = 1
        if j < 3:
            mk[:, 1, j + 1] = 1
        m["masks"] = mk
        maps.append(m)
    return maps


def gather_out(results):
    out = np.zeros((2, 8192, D), np.float32)
    for r in range(8):
        b, j = r // 4, r % 4
        xo = np.asarray(results[r]["xout"])
        out[b, j * NLAT:(j + 1) * NLAT] = xo.transpose(2, 1, 0).reshape(NLAT, D)
    return out


SETUP_KEYS = ["xin", "cT", "wmod", "bmodT", "n12T"]
COMMON_KEYS = ["gains", "convw", "sgn", "bsT", "wsT", "tabs", "masks"]


def _launch(seq, maps, state, nlast=L):
    nc = build(seq, nlast)
    has_setup = seq == "fused" or "setup" in seq
    wk = set()
    if seq == "fused":
        wk.update(WSPEC.keys())
    for t in seq:
        if t[0] == "A":
            wk.add("WA")
        if t[0] == "C":
            wk.update(["WC2", "WM", "WO", "W13", "W2"])
    ims = []
    for r in range(8):
        m = {k: maps[r][k] for k in COMMON_KEYS}
        for k in wk:
            m[k] = maps[r][k]
        if has_setup:
            for k in SETUP_KEYS:
                m[k] = maps[r][k]
        else:
            for k, v in state[r].items():
                m["st_" + k] = v
        ims.append(m)
    res = run_bass_kernel_spmd(nc, ims, core_ids=list(range(8)))
    return res.results


def _exchange_host(results):
    state = []
    for r in range(8):
        g = (r // 4) * 4
        o = results[r]
        ar = np.array(o["o_ARENA"])
        snds = [np.asarray(results[g + j]["o_SND"]) for j in range(4)]
        ar[:, 0:8192] = np.concatenate([sd[:, 0:2048] for sd in snds], 1)
        ar[:, 8448:8448 + 12288] = np.concatenate([sd[:, 2048:5120] for sd in snds], 1)
        halo = np.stack([sd[:, 5120:5128] for sd in snds], 1)
        state.append({"X": np.asarray(o["o_X"]), "QA": np.asarray(o["o_QA"]), "ARENA": ar, "ZB": np.asarray(o["o_ZB"]),
                      "modT": np.asarray(o["o_modT"]), "SB12": np.asarray(o["o_SB12"]), "HALO": np.ascontiguousarray(halo)})
    return state


def kernel_multi(inputs, nlast=L):
    maps = make_in_maps(inputs)
    seqs = [["setup", "A0"]] + [[f"B{l}", f"C{l}", f"A{l + 1}"] for l in range(nlast - 1)] + [[f"B{nlast - 1}", f"C{nlast - 1}"]]
    state = None
    for seq in seqs:
        results = _launch(seq, maps, state, nlast)
        if seq is not seqs[-1]:
            state = _exchange_host(results)
    return gather_out(results)


def kernel_fused(inputs, nlast=L):
    maps = make_in_maps(inputs)
    results = _launch("fused", maps, None, nlast)
    return gather_out(results)


def kernel(**inputs):
    return kernel_fused(inputs)
```

```python
import numpy as np
from contextlib import ExitStack
import concourse.bass as bass
import concourse.mybir as mybir
from concourse.bass_utils import run_bass_kernel_spmd

F32 = mybir.dt.float32
BF16 = mybir.dt.bfloat16
AF = mybir.ActivationFunctionType
ALU = mybir.AluOpType

L = 4
D = 1024
NT = 2304
NLAT = 2048
EPS = 1e-6
BLOCKS = [(0, 512), (512, 512), (1024, 512), (1536, 512), (2048, 256)]
SLOT = 5632
NSLOT = 2
NOCC = False
WSPEC = {"WA": (8, 15), "WC2": (8, 10), "WM": (32, 8), "WO": (8, 8), "W13": (8, 44), "W2": (22, 8)}
ENGS = ["pe", "act", "dve", "pool", "sp"]


class Res:
    __slots__ = ("w", "rd", "name")

    def __init__(self, name=""):
        self.w = None
        self.rd = []
        self.name = name


class Prog:
    def __init__(self, nc, es):
        self.nc = nc
        self.es = es
        self.ops = {e: [] for e in ENGS}
        self.sems = {}
        self.cnt = {}
        self.waited = {}
        for e in ENGS:
            self.sem("e_" + e)

    def sem(self, name):
        if name not in self.sems:
            self.sems[name] = self.es.enter_context(self.nc.semaphore(name))
            self.cnt[name] = 0
        return name

    def emit(self, eng, fn, rd=(), wr=(), sig=None, inc=1, extra_deps=(), also=()):
        deps = set(extra_deps)
        for w in also:
            if w.w is not None:
                deps.add(w.w)
            deps.update(w.rd)
        for r in rd:
            if r.w is not None:
                deps.add(r.w)
        for w in wr:
            if w.w is not None:
                deps.add(w.w)
            deps.update(w.rd)
        waits = []
        for (sn, val) in sorted(deps):
            if eng == "pe" and sn == "e_pe":
                continue
            key = (eng, sn)
            if self.waited.get(key, 0) >= val:
                continue
            self.waited[key] = val
            waits.append((sn, val))
        ev = None
        if sig is not False:
            sn = sig if sig else "e_" + eng
            self.cnt[sn] += inc
            ev = (sn, self.cnt[sn])
        self.ops[eng].append((waits, fn, (ev[0], inc) if ev else None))
        if ev:
            for r in rd:
                r.rd.append(ev)
            for w in wr:
                w.w = ev
                w.rd = []
        return ev

    def barrier(self):
        evs = [("e_" + e, self.cnt["e_" + e]) for e in ["pe", "act", "dve"] if self.cnt["e_" + e] > 0]
        for e in ["pe", "act", "dve", "sp", "pool"]:
            self.emit(e, None, extra_deps=[x for x in evs if x[0] != "e_" + e], sig=False)

    def replay(self, block):
        nc = self.nc
        sems = self.sems

        def run(eng_name):
            def body(eng):
                for waits, fn, sig in self.ops[eng_name]:
                    for sn, val in waits:
                        eng.wait_ge(sems[sn], val)
                    if fn is None:
                        continue
                    ins = fn(eng)
                    if sig:
                        ins.then_inc(sems[sig[0]], sig[1])
            return body
        block.tensor(run("pe"))
        block.scalar(run("act"))
        block.vector(run("dve"))
        block.gpsimd(run("pool"))
        block.sync(run("sp"))


def build(seq, nlast=L):
    fused = seq == 'fused'
    if fused:
        seq = ['setup'] + [t for l in range(nlast) for t in (f'A{l}', f'X{l}', f'B{l}', f'C{l}')]
    need = {}
    for t in seq:
        if t[0] == 'A':
            need.setdefault(int(t[1:]), set()).add('WA')
        if t[0] == 'C':
            need.setdefault(int(t[1:]), set()).update(['WC2', 'WM', 'WO', 'W13', 'W2'])
    has_setup = 'setup' in seq
    is_final = seq[-1] == f'C{nlast - 1}'
    nc = bass.Bass("TRN2", target_bir_lowering=False)
    es = ExitStack()
    with es:
        P = Prog(nc, es)

        def din(name, shape, dt=F32):
            return nc.dram_tensor(name, list(shape), dt, kind="ExternalInput").ap()

        if has_setup:
            xin = din("xin", [128, 8, NT])
            cT_d = din("cT", [128, 8, 2])
            wmod_d = din("wmod", [L, D, 1536])
            bmod_d = din("bmodT", [128, L, 12])
            n12_d = din("n12T", [128, L, 2, 8])
        gain_d = din("gains", [128, L, 4])
        convw_d = din("convw", [128, L, 2, 3])
        sgn_d = din("sgn", [128, L, 256])
        bs_d = din("bsT", [128, L, 2, 128])
        wsT_d = din("wsT", [128, L, 4, 128])
        tab_d = din("tabs", [128, 2, NLAT])
        mask_d = din("masks", [128, 2, 4])
        wd = {k: din(k, [L, 128, WSPEC[k][1] * WSPEC[k][0] * 128]) for k in sorted(set().union(*need.values()))}
        xout = nc.dram_tensor("xout", [128, 8, NLAT], F32, kind="ExternalOutput").ap() if is_final else None
        wsc = {(l, k): nc.dram_tensor(f"wsc_{k}_{l}", [128, WSPEC[k][1] * WSPEC[k][0] * 128], BF16)
               for l in need for k in need[l]}
        sndk_d = nc.dram_tensor("sndk_d", [128, 2048], BF16)
        gatk_d = nc.dram_tensor("gatk_d", [512, 2048], BF16)
        sndv_d = nc.dram_tensor("sndv_d", [128, 3088], BF16)
        gatv_d = nc.dram_tensor("gatv_d", [512, 3088], BF16)

        def sb(name, shape, dt):
            return es.enter_context(nc.sbuf_tensor(name, list(shape), dt))

        X = sb("X", [128, 8, NT], F32)
        QA = sb("QA", [128, 4, NT], BF16)
        ARENA = sb("ARENA", [128, 21120], BF16)
        SND = sb("SND", [128, 5136], BF16)
        WR = sb("WR", [128, NSLOT, SLOT], BF16)
        HT = sb("HT", [128, 8, 512], BF16)
        NTS = 5
        TS = sb("TS", [128, NTS, 512], F32)
        SQ = sb("SQ", [128, 2, 512], BF16)
        RS = sb("RS", [128, 512], F32)
        PS_ = sb("PS", [128, 2, 2, 512], BF16)
        ZE = sb("ZE", [128, 2, 516], F32)
        ONES = sb("ONES", [128, 128], BF16)
        BONES = sb("BONES", [128, 128], BF16)
        cT = sb("cTs", [128, 8, 2], F32)
        scT = sb("scT", [128, 8, 2], F32)
        bmod = sb("bmod", [128, L, 12], F32)
        MODP = sb("MODP", [128, L, 12, 2], F32)
        n12 = sb("n12", [128, L, 2, 8], F32)
        gains = sb("gainss", [128, L, 4], F32)
        convw = sb("convws", [128, L, 2, 3], F32)
        sgn = sb("sgns", [128, 256], F32)
        bsT = sb("bsTs", [128, 2, 128], F32)
        wsT = sb("wsTs", [128, 4, 128], BF16)
        masks = sb("maskss", [128, 2, 4], F32)
        modT = sb("modT", [128, L, 48, 2], F32)
        SB12 = sb("SB12", [128, L, 2, 8, 2], F32)
        ZB = sb("ZB", [128, 5, 2, 2], F32)
        HALO = sb("HALO", [128, 4, 8], BF16)
        HSEL = sb("HSEL", [128, 2, 4], F32)
        HTMP = sb("HTMP", [128, 4, 4], F32)
        SS1 = sb("SS1", [128, 4], F32)
        PSB = es.enter_context(nc.psum_tensor("PSB", [128, 8, 512], F32))
        banks = [PSB[:, i, :] for i in range(8)]
        RB = [Res(f"bank{i}") for i in range(8)]

        Kall = ARENA[:, 0:8448]
        Vall = ARENA[:, 8448:21120].rearrange("p (t c) -> p t c", c=192)
        TAB = ARENA[:, 8448:8448 + 8192].bitcast(F32).rearrange("p (a n) -> p a n", a=2)
        STG = ARENA[:, 0:16384].bitcast(F32).rearrange("p (s k n) -> p s k n", s=2, k=8)
        ACTB = ARENA[:, 0:11264].rearrange("p (k n) -> p k n", n=512)
        MB = ARENA[:, 11264:15360].rearrange("p (k n) -> p k n", n=512)
        ABB = ARENA[:, 15360:16384].rearrange("p (k n) -> p k n", n=512)
        UGB = ARENA[:, 16384:17408].rearrange("p (k n) -> p k n", n=512)
        YCB = ARENA[:, 17408:18432].rearrange("p (k n) -> p k n", n=512)
        YAB = ARENA[:, 18432:19456].rearrange("p (k n) -> p k n", n=512)
        SVNZ = ARENA[:, 19456:20480].rearrange("p (t g n) -> p t g n", t=2, g=4)

        R = {}

        def res(name):
            if name not in R:
                R[name] = Res(name)
            return R[name]
        RX = [res(f"x{b}") for b in range(5)]
        RQA = [[res(f"qa{c}_{b}") for b in range(5)] for c in range(4)]
        RT = [res(f"ts{i}") for i in range(NTS)]
        RSQ = [res("sq0"), res("sq1")]
        RPS = [res(f"ps{i}") for i in range(2)]
        RSLOT = [res(f"slot{i}") for i in range(NSLOT)]
        st = {"bank": 0, "ts": 0, "sq": 0, "slot": 0, "qz": 0, "pools": list(range(8))}

        def nbank():
            pool = st["pools"]
            b = pool[st["bank"] % len(pool)]
            st["bank"] += 1
            return b

        def nts():
            i = st["ts"] % NTS
            st["ts"] += 1
            return i

        def act(out, in_, func, rd, wr, **kw):
            return P.emit("act", lambda e: e.activation(out=out, in_=in_, func=func, **kw), rd, wr)

        def tt(out, in0, in1, op, rd, wr):
            return P.emit("dve", lambda e: e.tensor_tensor(out=out, in0=in0, in1=in1, op=op), rd, wr)

        def ptt(out, in0, in1, op, rd, wr):
            return P.emit("pool", lambda e: e.tensor_tensor(out=out, in0=in0, in1=in1, op=op), rd, wr)

        def stt(out, in0, scalar, in1, op0, op1, rd, wr):
            return P.emit("dve", lambda e: e.scalar_tensor_tensor(out=out, in0=in0, scalar=scalar, in1=in1, op0=op0, op1=op1), rd, wr)

        def ts_(out, in0, s1, s2, op0, op1, rd, wr):
            if op1 is None:
                return P.emit("dve", lambda e: e.tensor_scalar(out=out, in0=in0, scalar1=s1, scalar2=None, op0=op0), rd, wr)
            return P.emit("dve", lambda e: e.tensor_scalar(out=out, in0=in0, scalar1=s1, scalar2=s2, op0=op0, op1=op1), rd, wr)

        def rcp(out, in_, rd, wr):
            return P.emit("dve", lambda e: e.reciprocal(out=out, in_=in_), rd, wr)

        def cpy(out, in_, rd, wr):
            return P.emit("dve", lambda e: e.tensor_copy(out=out, in_=in_), rd, wr)

        def mset(ap, val, wr):
            return P.emit("dve", lambda e: e.memset(ap, val), (), wr)

        def mmg(mms, rd, wr):
            n = len(mms)
            for i, (o, a, b) in enumerate(mms):
                last = i == n - 1
                fn = (lambda o=o, a=a, b=b, i=i, last=last: (lambda e: e.matmul(o, lhsT=a, rhs=b, start=(i == 0), stop=last)))()
                if last:
                    P.emit("pe", fn, rd, wr)
                else:
                    P.emit("pe", fn, rd if i == 0 else (), (), sig=False, also=wr if i == 0 else ())

        def dma(q, out, in_, rd, wr, sem):
            P.sem(sem)
            return P.emit(q, lambda e: e.dma_start(out=out, in_=in_), rd, wr, sig=sem, inc=16)

        RSC = {k: res(f"wsc_{k[1]}_{k[0]}") for k in wsc}

        def convert_layer(l, only=None):
            for k in [k_ for k_ in ("WA", "WC2", "WM", "WO", "W13", "W2") if k_ in need.get(l, ()) and (only is None or k_ in only)]:
                kc, oc = WSPEC[k]
                tot = oc * kc * 128
                step = 8192 * 2
                for a in range(0, tot, step):
                    b = min(tot, a + step)
                    P.sem(f"cv{l}_{k}")
                    P.emit("pool", (lambda a=a, b=b, k=k: (lambda e: e.dma_start(out=wsc[(l, k)].ap()[:, a:b].rearrange("p (n c) -> p n c", c=1024), in_=wd[k][l, :, a:b].rearrange("p (n c) -> p n c", c=1024))))(),
                           (), [RSC[(l, k)]], sig=f"cv{l}_{k}", inc=16)

        def wload(l, k, oc0, n):
            kc, oc = WSPEC[k]
            s = st["slot"] % NSLOT
            st["slot"] += 1
            sz = n * kc * 128
            assert sz <= SLOT
            src = wsc[(l, k)].ap()[:, oc0 * kc * 128: oc0 * kc * 128 + sz]
            dma("sp", WR[:, s, 0:sz], src, [RSC[(l, k)]], [RSLOT[s]], f"ws{s}")
            return WR[:, s, 0:sz].rearrange("p (u k j) -> p u k j", u=n, k=kc), RSLOT[s]

        class Stream:
            def __init__(self, l):
                self.l = l
                self.items = []

            def add(self, k, oc0, n, fn):
                self.items.append((k, oc0, n, fn))

            def run(self):
                pend = None
                for i, (k, oc0, n, fn) in enumerate(self.items):
                    if pend is None:
                        pend = wload(self.l, k, oc0, n)
                    cur = pend
                    pend = None
                    if i + 1 < len(self.items) and NSLOT > 1:
                        k2, o2, n2, _ = self.items[i + 1]
                        pend = wload(self.l, k2, o2, n2)
                    fn(cur[0], cur[1])

        for (name, t, d) in [("gains", gains, gain_d), ("convw", convw, convw_d), ("masks", masks, mask_d)]:
            dma("sp", t[:], d, (), [res(name)], "ld_" + name)
        mset(ONES[:], 1.0, [res("ones")])
        mset(BONES[:], 0.0, [res("bones")])
        mset(BONES[0:64, 0:64], 1.0, [res("bones")])
        mset(BONES[64:128, 64:128], 1.0, [res("bones")])
        mset(SND[:, 2048:5136], 1.0, [res("snd_v")])
        STATE = [("X", X), ("QA", QA), ("ARENA", ARENA), ("ZB", ZB), ("modT", modT), ("SB12", SB12), ("HALO", HALO)]
        if has_setup:
            dma("sp", cT[:], cT_d, (), [res("cT")], "ld_cT")
            for (name, t, d) in [("bmod", bmod, bmod_d), ("n12", n12, n12_d)]:
                dma("sp", t[:], d, (), [res(name)], "ld_" + name)
            for b, (s0, n) in enumerate(BLOCKS):
                dma("sp", X[:, :, s0:s0 + n], xin[:, :, s0:s0 + n], (), [RX[b]], f"ldx{b}")
            mset(ZB[:], 0.0, [res("zb")])
        else:
            for nm, t in STATE:
                d = din("st_" + nm, list(t.shape), t.dtype)
                rs = {"X": RX, "QA": [r_ for rr in RQA for r_ in rr], "ARENA": [res("klat"), res("vlat"), res("kctx"), res("vctx")],
                      "ZB": [res("zb")], "modT": [res("modT")], "SB12": [res("sb12")], "HALO": [res("halo")]}[nm]
                dma("sp", t[:], d, (), rs, "lds_" + nm)
        for l in sorted(need):
            if not fused:
                convert_layer(l)
            elif l == 0:
                convert_layer(0, only=("WA",))

        if has_setup:
            act(scT[:], cT[:], AF.Silu, [res("cT")], [res("scT")])
            RSTG = [res("stg0"), res("stg1")]
            sndm_d = nc.dram_tensor("sndm_d", [128, L * 24], F32)
            gatm_d = nc.dram_tensor("gatm_d", [512, L * 24], F32)
            bk = nbank()
            for l in range(nlast):
                for j in range(3):
                    s = (l * 3 + j) % 2
                    dma("sp", STG[:, s], wmod_d[l, :, j * 512:(j + 1) * 512].rearrange("(k p) n -> p k n", p=128),
                        (), [RSTG[s]], f"stg{s}")
                    for oc in range(4):
                        col = l * 24 + (j * 4 + oc) * 2
                        mmg([(banks[bk][:, col:col + 2], STG[:, s, kc, oc * 128:(oc + 1) * 128], scT[:, kc, :]) for kc in range(8)],
                            [RSTG[s], res("scT")], [RB[bk]])
                for j in range(2):
                    tt(MODP[:, l, :, j], banks[bk][:, l * 24 + j:l * 24 + 24:2], bmod[:, l, :], ALU.add, [RB[bk], res("bmod")], [res("modp")])
            if nlast < L:
                mset(MODP[:, nlast:L], 0.0, [res("modp")])
            for nm in ("sxm", "ccm", "lgm"):
                P.sem(nm)
            P.emit("pool", lambda e: e.dma_start(out=sndm_d.ap(), in_=MODP[:].rearrange("p l o j -> p (l o j)")), [res("modp")], [res("sndm_d")], sig="sxm", inc=16)
            P.emit("pool", lambda e: e.collective_compute("AllGather", ALU.bypass, replica_groups=[[0, 1, 2, 3], [4, 5, 6, 7]],
                                                          ins=[sndm_d.ap().opt()], outs=[gatm_d.ap().opt()]),
                   [res("sndm_d")], [res("gatm_d")], sig="ccm", inc=1)
            for r in range(4):
                P.emit("pool", (lambda r=r: (lambda e: e.dma_start(out=modT[:, :, r * 12:(r + 1) * 12, :].rearrange("p l o j -> p l (o j)"),
                                                                   in_=gatm_d.ap()[r * 128:(r + 1) * 128, :].rearrange("p (l x) -> p l x", l=L))))(),
                       [res("gatm_d")], [res("modT")], sig="lgm", inc=16)
            for l in range(nlast):
                for nrm in range(2):
                    for j in range(2):
                        sc = modT[:, l, 8 + 24 * nrm:16 + 24 * nrm, j]
                        stt(SB12[:, l, nrm, :, j], sc, 1.0, n12[:, l, nrm, :], ALU.add, ALU.mult,
                            [res("modT"), res("n12")], [res("sb12")])
            P.barrier()

        def modv(l, which, kc, j):
            base = {"sh1": 0, "gt1": 16, "sh2": 24, "gt2": 40}[which]
            return modT[:, l, base + kc, j:j + 1]

        def norm_block(l, b, nrm):
            s0, n = BLOCKS[b]
            j = 1 if b == 4 else 0
            bk = nbank()
            mms = []
            for kc in range(8):
                q = st["sq"] % 2
                st["sq"] += 1
                act(SQ[:, q, 0:n], X[:, kc, s0:s0 + n], AF.Square, [RX[b]], [RSQ[q]])
                P.emit("pe", (lambda q=q, kc=kc: (lambda e: e.matmul(banks[bk][:, 0:n], lhsT=ONES[:], rhs=SQ[:, q, 0:n], start=(kc == 0), stop=(kc == 7))))(),
                       [RSQ[q], res("ones")], [RB[bk]] if kc == 7 else (), sig=None, also=[RB[bk]] if kc == 0 else ())
            t1 = nts()
            act(TS[:, t1, 0:n], banks[bk][:, 0:n], AF.Sqrt, [RB[bk]], [RT[t1]], scale=1.0 / D, bias=EPS)
            rcp(RS[:, 0:n], TS[:, t1, 0:n], [RT[t1]], [res("rs")])
            for kc in range(8):
                t3 = nts()
                if kc % 2 == 0:
                    tt(TS[:, t3, 0:n], X[:, kc, s0:s0 + n], RS[:, 0:n], ALU.mult, [RX[b], res("rs")], [RT[t3]])
                else:
                    ptt(TS[:, t3, 0:n], X[:, kc, s0:s0 + n], RS[:, 0:n], ALU.mult, [RX[b], res("rs")], [RT[t3]])
                act(HT[:, kc, 0:n], TS[:, t3, 0:n], AF.Identity, [RT[t3], res("sb12"), res("modT")], [res(f"ht{kc}")],
                    scale=SB12[:, l, nrm, kc, j:j + 1], bias=modv(l, "sh1" if nrm == 0 else "sh2", kc, j))

        RHT = [res(f"ht{kc}") for kc in range(8)]

        def proj_fm(wv, wr_, u, n, cols=None):
            bk = nbank()
            if cols is None:
                mmg([(banks[bk][:, 0:n], wv[:, u, kc, :], HT[:, kc, 0:n]) for kc in range(8)], [wr_] + RHT, [RB[bk]])
            else:
                mmg([(banks[bk][:, cols[0]:cols[0] + 2], wv[:, u, kc, :], HT[:, kc, 0:n:n - 1]) for kc in range(8)], [wr_] + RHT, [RB[bk]])
            return bk

        def qk_finish(l, b, bq, bqr, gi, out_ap, out_res):
            s0, n = BLOCKS[b]
            q = st["sq"] % 2
            st["sq"] += 1
            act(SQ[:, q, 0:n], banks[bq][:, 0:n], AF.Square, [RB[bq]], [RSQ[q]])
            bs = nbank()
            mmg([(banks[bs][:, 0:n], BONES[:], SQ[:, q, 0:n])], [RSQ[q], res("bones")], [RB[bs]])
            t1 = nts()
            act(TS[:, t1, 0:n], banks[bs][:, 0:n], AF.Sqrt, [RB[bs]], [RT[t1]], scale=1.0 / 64, bias=EPS)
            t2 = nts()
            rcp(TS[:, t2, 0:n], TS[:, t1, 0:n], [RT[t1]], [RT[t2]])
            g = gains[:, l, gi:gi + 1]
            if bqr is None:
                stt(out_ap, banks[bq][:, 0:n], g, TS[:, t2, 0:n], ALU.mult, ALU.mult, [RB[bq], RT[t2], res("gains")], [out_res])
                return
            gp = gains[:, l, gi + 1:gi + 2]
            t3 = nts()
            stt(TS[:, t3, 0:n], banks[bq][:, 0:n], g, TAB[:, 0, s0:s0 + n], ALU.mult, ALU.mult, [RB[bq], res("vlat"), res("gains")], [RT[t3]])
            t4 = nts()
            stt(TS[:, t4, 0:n], banks[bqr][:, 0:n], gp, TAB[:, 1, s0:s0 + n], ALU.mult, ALU.mult, [RB[bqr], res("vlat"), res("gains")], [RT[t4]])
            ptt(TS[:, t3, 0:n], TS[:, t3, 0:n], TS[:, t4, 0:n], ALU.add, [RT[t3], RT[t4]], [RT[t3]])
            ptt(out_ap, TS[:, t3, 0:n], TS[:, t2, 0:n], ALU.mult, [RT[t3], RT[t2]], [out_res])

        def phase_A(l):
            last = l == nlast - 1
            dma("sp", TAB[:], tab_d, (), [res("vlat")], "ldtab")
            for b, (s0, n) in enumerate(BLOCKS):
                ctx = b == 4
                norm_block(l, b, 0)
                v0, r0 = wload(l, "WA", 0, 5)
                v1, r1 = wload(l, "WA", 5, 5)

                def unit(u):
                    return (v0, r0, u) if u < 5 else ((v1, r1, u - 5) if u < 10 else None)
                if not (ctx and last):
                    for c in range(4):
                        bq = proj_fm(v0, r0, c, n)
                        if ctx:
                            qk_finish(l, b, bq, None, 0, QA[:, c, s0:s0 + n], RQA[c][b])
                        else:
                            ur = 4 + c
                            vv, rr, uu = unit(ur)
                            bqr = proj_fm(vv, rr, uu, n)
                            qk_finish(l, b, bq, bqr, 0, QA[:, c, s0:s0 + n], RQA[c][b])
                bq = proj_fm(v1, r1, 3, n)
                if ctx:
                    qk_finish(l, b, bq, None, 2, Kall[:, 8192:8448], res("kctx"))
                else:
                    bqr = proj_fm(v1, r1, 4, n)
                    qk_finish(l, b, bq, bqr, 2, SND[:, s0:s0 + n], res("snd_k"))
                v2, r2 = wload(l, "WA", 10, 5)
                bk = nbank()
                nt_ = n // 128
                for t in range(nt_):
                    mmg([(banks[bk][:, t * 128:(t + 1) * 128], HT[:, kc, t * 128:(t + 1) * 128], v2[:, 0, kc, :]) for kc in range(8)],
                        [r2] + RHT, [RB[bk]])
                pv = banks[bk][:, 0:nt_ * 128].rearrange("p (t c) -> p t c", c=128)
                if ctx:
                    dst = Vall[:, 64:66, :]
                    rs = res("vctx")
                else:
                    dst = SND[:, 2048:5120].rearrange("p (t c) -> p t c", c=192)[:, (s0 // 128):(s0 // 128) + nt_, :]
                    rs = res("snd_v")
                act(dst[:, :, 0:64], pv[:, :, 0:64], AF.Copy, [RB[bk]], [rs])
                act(dst[:, :, 128:192], pv[:, :, 64:128], AF.Copy, [RB[bk]], [rs])
                if ctx:
                    P.emit("dve", lambda e: e.memset(Vall[:, 64:66, 64:128], 1.0), (), [res("vctx")])
                if not ctx:
                    bk = nbank()
                    for i, u in enumerate([1, 2, 3, 4]):
                        mmg([(banks[bk][:, 2 * i:2 * i + 2], v2[:, u, kc, :], HT[:, kc, 0:n:n - 1]) for kc in range(8)], [r2] + RHT, [RB[bk]])
                    t1 = nts()
                    act(TS[:, t1, 0:4], banks[bk][:, 0:4], AF.Copy, [RB[bk]], [RT[t1]])
                    tt(ZB[:, b, :, :], TS[:, t1, 0:4].rearrange("p (c f) -> p c f", f=2),
                       banks[bk][:, 4:8].rearrange("p (c f) -> p c f", f=2), ALU.mult, [RB[bk], RT[t1]], [res("zb")])
                    if b == 0:
                        cpy(SND[:, 5120:5124:2], ZB[:, 0, :, 0], [res("zb")], [res("snd_h")])
                    if b == 3:
                        cpy(SND[:, 5121:5125:2], ZB[:, 3, :, 1], [res("zb")], [res("snd_h")])

        def exchange(l):
            for nm in ("sxk", "sxv", "cck", "ccv", "lg0", "lg1", "lg2"):
                P.sem(nm)
            rg = [[0, 1, 2, 3], [4, 5, 6, 7]]
            P.emit("pool", lambda e: e.dma_start(out=sndk_d.ap(), in_=SND[:, 0:2048]), [res("snd_k")], [res("sndk_d")], sig="sxk", inc=16)
            P.emit("pool", lambda e: e.collective_compute("AllGather", ALU.bypass, replica_groups=rg, ins=[sndk_d.ap().opt()], outs=[gatk_d.ap().opt()]),
                   [res("sndk_d")], [res("gatk_d")], sig="cck", inc=1)
            P.emit("pool", lambda e: e.dma_start(out=sndv_d.ap(), in_=SND[:, 2048:5136]), [res("snd_v"), res("snd_h")], [res("sndv_d")], sig="sxv", inc=16)
            P.emit("pool", lambda e: e.collective_compute("AllGather", ALU.bypass, replica_groups=rg, ins=[sndv_d.ap().opt()], outs=[gatv_d.ap().opt()]),
                   [res("sndv_d")], [res("gatv_d")], sig="ccv", inc=1)
            gk = gatk_d.ap().rearrange("(r p) c -> p r c", p=128)
            gv = gatv_d.ap().rearrange("(r p) c -> p r c", p=128)
            P.emit("pool", lambda e: e.dma_start(out=Kall[:, 0:8192].rearrange("p (r c) -> p r c", r=4), in_=gk),
                   [res("gatk_d")], [res("klat")], sig="lg0", inc=16)
            P.emit("pool", lambda e: e.dma_start(out=ARENA[:, 8448:8448 + 12288].rearrange("p (r c) -> p r c", r=4), in_=gv[:, :, 0:3072]),
                   [res("gatv_d")], [res("vlat")], sig="lg1", inc=16)
            P.emit("pool", lambda e: e.dma_start(out=HALO[:], in_=gv[:, :, 3072:3080]),
                   [res("gatv_d")], [res("halo")], sig="lg2", inc=16)
            if fused and l == 0:
                convert_layer(0, only=("WC2", "WM", "WO", "W13", "W2"))
            if fused and l + 1 < nlast:
                convert_layer(l + 1)

        def halo_select():
            for side in range(2):
                tt(HTMP[:], HALO[:, :, 0:4], masks[:, side, :].unsqueeze(2).to_broadcast([128, 4, 4]), ALU.mult,
                   [res("halo"), res("masks")], [res("htmp")])
                P.emit("dve", (lambda side=side: (lambda e: e.tensor_reduce(out=HSEL[:, side, :], in_=HTMP[:].rearrange("p r x -> p x r"),
                                                                            axis=mybir.AxisListType.X, op=ALU.add)))(),
                       [res("htmp")], [res("hsel")])

        def phase_B(l):
            last = l == nlast - 1
            for b, (s0, n) in [(4, BLOCKS[4])] + list(enumerate(BLOCKS[:4])):
                ctx = b == 4
                if ctx and last:
                    continue
                kts = [64, 65] if ctx else list(range(66))
                for c in range(4):
                    oa, ob = ((0, 1), (6, 7))[st["qz"] % 2]
                    st["qz"] += 1
                    sb_ = {}

                    def s_mm(kt, i):
                        ba = (2, 4)[i % 2]
                        bb = ba + 1
                        rk = res("kctx") if kt >= 64 else res("klat")
                        mmg([(banks[ba][:, 0:n], Kall[0:64, kt * 128:(kt + 1) * 128], QA[0:64, c, s0:s0 + n])], [rk, RQA[c][b]], [RB[ba]])
                        mmg([(banks[bb][:, 0:n], Kall[64:128, kt * 128:(kt + 1) * 128], QA[64:128, c, s0:s0 + n])], [rk, RQA[c][b]], [RB[bb]])
                        sb_[kt] = ba
                    s_mm(kts[0], 0)
                    for i, kt in enumerate(kts):
                        if i + 1 < len(kts):
                            s_mm(kts[i + 1], i + 1)
                        ba = sb_.pop(kt)
                        pp = i % 2
                        act(PS_[:, pp, :, 0:n], PSB[:, ba:ba + 2, 0:n], AF.Exp, [RB[ba], RB[ba + 1]], [RPS[pp]], scale=0.125)
                        rv = res("vctx") if kt >= 64 else res("vlat")
                        fst, lst = i == 0, i == len(kts) - 1
                        for (o, h, lo) in ((oa, 0, 0), (ob, 1, 64)):
                            P.emit("pe", (lambda o=o, h=h, pp=pp, lo=lo, kt=kt, fst=fst, lst=lst, n=n:
                                          (lambda e: e.matmul(banks[o][:, 0:n], lhsT=Vall[:, kt, lo:lo + 128], rhs=PS_[:, pp, h, 0:n], start=fst, stop=lst)))(),
                                   [rv, RPS[pp]], [RB[o]] if lst else (), sig=None, also=[RB[o]] if fst else ())
                    t1 = nts()
                    rcp(TS[:, t1, 0:n], banks[oa][:, 0:n], [RB[oa]], [RT[t1]])
                    tt(QA[0:64, c, s0:s0 + n], banks[oa][0:64, 0:n], TS[64:128, t1, 0:n], ALU.mult, [RB[oa], RT[t1]], [RQA[c][b]])
                    t2 = nts()
                    rcp(TS[:, t2, 0:n], banks[ob][:, 0:n], [RB[ob]], [RT[t2]])
                    tt(QA[64:128, c, s0:s0 + n], banks[ob][64:128, 0:n], TS[0:64, t2, 0:n], ALU.mult, [RB[ob], RT[t2]], [RQA[c][b]])

        def phase_C(l):
            last = l == nlast - 1
            pre = {}
            P.sem("ldw")
            P.emit("pool", lambda e: e.dma_start(out=wsT[:], in_=wsT_d[:, l]), (), [res("wsT")], sig="ldw", inc=16)
            dma("sp", sgn[:], sgn_d[:, l, :], (), [res("sgn")], "ldp0")
            dma("sp", bsT[:], bs_d[:, l], (), [res("bsT")], "ldp1")
            for b, (s0, n) in enumerate(BLOCKS):
                ctx = b == 4
                if ctx and last:
                    continue
                j = 1 if ctx else 0
                nt_ = n // 128
                if not pre.get(b):
                    norm_block(l, b, 0)
                nxt = [bb for bb in range(b + 1, 5) if not (bb == 4 and last)]
                nb = nxt[0] if nxt else None
                v0, r0 = wload(l, "WC2", 0, 5)
                v1, r1 = wload(l, "WC2", 5, 5)

                def un(u):
                    return (v0, r0, u) if u < 5 else (v1, r1, u - 5)
                def do_ab(ch):
                    bk = proj_fm(*un(ch), n)
                    act(ABB[:, ch, 0:n], banks[bk][:, 0:n], AF.Copy, [RB[bk]], [res(f"ab{ch}")])

                def do_acax(ch):
                    bk = proj_fm(*un(2 + ch), n)
                    t1 = nts()
                    act(TS[:, t1, 0:n], banks[bk][:, 0:n], AF.Copy, [RB[bk]], [RT[t1]])
                    bk2 = proj_fm(*un(4 + ch), n)
                    rz = res(f"ze{ch}")
                    tt(ZE[:, ch, 1:n + 1], banks[bk2][:, 0:n], TS[:, t1, 0:n], ALU.mult, [RB[bk2], RT[t1]], [rz])
                    if ctx:
                        mset(ZE[:, ch, 0:1], 0.0, [rz])
                        mset(ZE[:, ch, n + 1:n + 2], 0.0, [rz])
                    else:
                        if b == 0:
                            cpy(ZE[:, ch, 0:1], HSEL[:, 0, 2 * ch + 1:2 * ch + 2], [res("hsel")], [rz])
                        else:
                            cpy(ZE[:, ch, 0:1], ZB[:, b - 1, ch, 1:2], [res("zb")], [rz])
                        if b == 3:
                            cpy(ZE[:, ch, n + 1:n + 2], HSEL[:, 1, 2 * ch:2 * ch + 1], [res("hsel")], [rz])
                        else:
                            cpy(ZE[:, ch, n + 1:n + 2], ZB[:, b + 1, ch, 0:1], [res("zb")], [rz])
                    t2 = nts()
                    ts_(TS[:, t2, 0:n], ZE[:, ch, 1:n + 1], convw[:, l, ch, 1:2], None, ALU.mult, None, [rz, res("convw")], [RT[t2]])
                    stt(TS[:, t2, 0:n], ZE[:, ch, 0:n], convw[:, l, ch, 0:1], TS[:, t2, 0:n], ALU.mult, ALU.add, [rz, RT[t2]], [RT[t2]])
                    stt(TS[:, t2, 0:n], ZE[:, ch, 2:n + 2], convw[:, l, ch, 2:3], TS[:, t2, 0:n], ALU.mult, ALU.add, [rz, RT[t2]], [RT[t2]])
                    tt(YAB[:, ch, 0:n], TS[:, t2, 0:n], ABB[:, ch, 0:n], ALU.mult, [RT[t2], res(f"ab{ch}")], [res(f"ya{ch}")])

                def do_u(ch):
                    bk = proj_fm(*un(6 + ch), n)
                    act(UGB[:, ch, 0:n], banks[bk][:, 0:n], AF.Gelu, [RB[bk]], [res(f"ug{ch}")])

                mset(SVNZ[:], 0.0, [res("svn0"), res("svn1")])
                mixb = [nbank(), nbank()]
                st["pools"] = [i for i in range(8) if i not in mixb]

                def do_sv(t):
                    bk = nbank()
                    for uu in range(2):
                        vv, rr, u_ = un(8 + uu)
                        mmg([(banks[bk][:, uu * 128:(uu + 1) * 128], HT[:, kc, t * 128:(t + 1) * 128], vv[:, u_, kc, :]) for kc in range(8)],
                            [rr] + RHT, [RB[bk]])
                    t1 = nts()
                    act(TS[:, t1, 0:256], banks[bk][:, 0:256], AF.Gelu, [RB[bk]], [RT[t1]])
                    t2 = nts()
                    act(TS[:, t2, 0:256], TS[:, t1, 0:256], AF.Square, [RT[t1]], [RT[t2], res("ss1")], accum_out=SS1[:, t:t + 1])
                    act(SS1[:, t:t + 1], SS1[:, t:t + 1], AF.Sqrt, [res("ss1")], [res("ss1")], scale=1.0 / 256, bias=EPS)
                    P.emit("dve", (lambda t=t: (lambda e: e.reciprocal(out=SS1[:, t:t + 1], in_=SS1[:, t:t + 1])))(), [res("ss1")], [res("ss1")])
                    tb = t % 2
                    rsv = res(f"svn{tb}")
                    for g in range(4):
                        gl = g % 2
                        stt(SVNZ[:, tb, g, gl * 64:(gl + 1) * 64], TS[:, t1, g * 64:(g + 1) * 64], SS1[:, t:t + 1], sgn[:, g * 64:(g + 1) * 64],
                            ALU.mult, ALU.mult, [RT[t1], res("ss1"), res("sgn")], [rsv])

                def do_mix(t):
                    tb = t % 2
                    for ch in range(2):
                        mmg([(banks[mixb[ch]][:, t * 128:(t + 1) * 128], SVNZ[:, tb, 2 * ch + gl, :], wsT[:, 2 * ch + gl, :]) for gl in range(2)],
                            [res(f"svn{tb}"), res("wsT")], [RB[mixb[ch]]])

                do_sv(0); do_sv(1); do_ab(0); do_ab(1); do_mix(0); do_mix(1)
                if nt_ == 4:
                    do_sv(2); do_sv(3); do_acax(0); do_mix(2); do_acax(1); do_mix(3)
                else:
                    do_acax(0); do_acax(1)
                do_u(0); do_u(1)
                st["pools"] = list(range(8))
                for ch in range(2):
                    bk = mixb[ch]
                    t1 = nts()
                    for t in range(nt_):
                        tt(TS[:, t1, t * 128:(t + 1) * 128], banks[bk][:, t * 128:(t + 1) * 128], bsT[:, ch, :], ALU.add, [RB[bk], res("bsT")], [RT[t1]])
                    tt(YCB[:, ch, 0:n], TS[:, t1, 0:n], UGB[:, ch, 0:n], ALU.mult, [RT[t1], res(f"ug{ch}")], [res(f"yc{ch}")])
                S = Stream(l)
                for oc in range(8):
                    def fm(wv, wr_, oc=oc):
                        srcs = [(0, 2, YAB, [res("ya0"), res("ya1")]), (2, 4, QA, None), (6, 2, YCB, [res("yc0"), res("yc1")])]
                        tacc = None
                        for bi, (k0, nk, src, rs) in enumerate(srcs):
                            bg = nbank()
                            mmg([(banks[bg][:, 0:n], wv[:, 0, 8 + 8 * bi + kc, :], HT[:, kc, 0:n]) for kc in range(8)], [wr_] + RHT, [RB[bg]])
                            by = nbank()
                            if src is QA:
                                mmg([(banks[by][:, 0:n], wv[:, 0, k0 + kc, :], QA[:, kc, s0:s0 + n]) for kc in range(nk)],
                                    [wr_] + [RQA[kc][b] for kc in range(4)], [RB[by]])
                            else:
                                mmg([(banks[by][:, 0:n], wv[:, 0, k0 + kc, :], src[:, kc, 0:n]) for kc in range(nk)], [wr_] + rs, [RB[by]])
                            t1 = nts()
                            act(TS[:, t1, 0:n], banks[bg][:, 0:n], AF.Sigmoid, [RB[bg]], [RT[t1]])
                            tt(TS[:, t1, 0:n], banks[by][:, 0:n], TS[:, t1, 0:n], ALU.mult, [RB[by], RT[t1]], [RT[t1]])
                            if tacc is None:
                                tacc = t1
                            elif bi == 1:
                                tt(TS[:, tacc, 0:n], TS[:, tacc, 0:n], TS[:, t1, 0:n], ALU.add, [RT[tacc], RT[t1]], [RT[tacc]])
                            else:
                                tt(MB[:, oc, 0:n], TS[:, tacc, 0:n], TS[:, t1, 0:n], ALU.add, [RT[tacc], RT[t1]], [res(f"m{oc}")])
                    S.add("WM", oc, 1, fm)
                RM = [res(f"m{k}") for k in range(8)]

                def make_fo(o0, cnt):
                    def fo(wv, wr_):
                        for u in range(cnt):
                            oc = o0 + u
                            bk = nbank()
                            mmg([(banks[bk][:, 0:n], wv[:, u, kc, :], MB[:, kc, 0:n]) for kc in range(8)], [wr_] + RM, [RB[bk]])
                            stt(X[:, oc, s0:s0 + n], banks[bk][:, 0:n], modv(l, "gt1", oc, j), X[:, oc, s0:s0 + n], ALU.mult, ALU.add,
                                [RB[bk], res("modT"), RX[b]], [RX[b]])
                    return fo
                S.add("WO", 0, 4, make_fo(0, 4))
                S.add("WO", 4, 4, make_fo(4, 4))
                S.run()
                norm_block(l, b, 1)
                S = Stream(l)

                def make_f13(j0, cnt):
                    def f13(wv, wr_):
                        for u in range(cnt):
                            jj = j0 + u
                            b1 = nbank()
                            mmg([(banks[b1][:, 0:n], wv[:, 2 * u, kc, :], HT[:, kc, 0:n]) for kc in range(8)], [wr_] + RHT, [RB[b1]])
                            b3 = nbank()
                            mmg([(banks[b3][:, 0:n], wv[:, 2 * u + 1, kc, :], HT[:, kc, 0:n]) for kc in range(8)], [wr_] + RHT, [RB[b3]])
                            t1 = nts()
                            act(TS[:, t1, 0:n], banks[b1][:, 0:n], AF.Silu, [RB[b1]], [RT[t1]])
                            tt(ACTB[:, jj, 0:n], banks[b3][:, 0:n], TS[:, t1, 0:n], ALU.mult, [RB[b3], RT[t1]], [res(f"act{jj}")])
                    return f13
                for j0 in range(0, 22, 2):
                    S.add("W13", 2 * j0, 4, make_f13(j0, 2))
                RACT = [res(f"act{k}") for k in range(22)]

                def make_f2(o0):
                    def f2(wv, wr_):
                        if o0 == 4 and nb is not None:
                            norm_block(l, nb, 0)
                            pre[nb] = True
                        for u in range(2):
                            oc = o0 + u
                            bk = nbank()
                            mmg([(banks[bk][:, 0:n], wv[:, u, kc, :], ACTB[:, kc, 0:n]) for kc in range(22)], [wr_] + RACT, [RB[bk]])
                            stt(X[:, oc, s0:s0 + n], banks[bk][:, 0:n], modv(l, "gt2", oc, j), X[:, oc, s0:s0 + n], ALU.mult, ALU.add,
                                [RB[bk], res("modT"), RX[b]], [RX[b]])
                    return f2
                for o0 in range(0, 8, 2):
                    S.add("W2", o0, 2, make_f2(o0))
                S.run()
                if last:
                    P.sem("st")
                    dma("sp", xout[:, :, s0:s0 + n], X[:, :, s0:s0 + n], [RX[b]], [res("xout")], "st")

        for t in seq:
            if t == 'setup':
                continue
            l = int(t[1:])
            if t[0] == 'A':
                phase_A(l)
            elif t[0] == 'X':
                exchange(l)
            elif t[0] == 'B':
                halo_select()
                phase_B(l)
                P.barrier()
            elif t[0] == 'C':
                phase_C(l)
                P.barrier()
        if not is_final:
            P.barrier()
            for nm, t in STATE[:-1] + [("SND", SND)]:
                o = nc.dram_tensor("o_" + nm, list(t.shape), t.dtype, kind="ExternalOutput").ap()
                dma("sp", o, t[:], [], [res("dbgout")], "st")
        P.sem("st")
        P.emit("sp", None, extra_deps=[("st", P.cnt["st"])], sig=False)
        block = es.enter_context(nc.Block())
        P.replay(block)
    return nc


HD = 64


def _partner():
    d = np.arange(64)
    half = (d % 32) // 16
    return np.where(half == 0, d + 16, d - 16)


def _unit_layout(w):
    K, N = w.shape
    kc, oc = K // 128, N // 128
    return np.ascontiguousarray(w.reshape(kc, 128, oc, 128).transpose(1, 2, 0, 3)).reshape(128, oc * kc * 128)


def _prep_shared(inp):
    f = np.float32
    w_in = np.asarray(inp["w_in"], f)
    pt = _partner()
    OFF_Q, OFF_K, OFF_V, OFF_U, OFF_SV, OFF_G = 768, 1280, 1408, 1536, 1792, 2048
    qcols = np.concatenate([np.concatenate([OFF_Q + c * 64 + np.arange(64), OFF_Q + (4 + c) * 64 + np.arange(64)]) for c in range(4)])
    qrcols = np.concatenate([np.concatenate([OFF_Q + c * 64 + pt, OFF_Q + (4 + c) * 64 + pt]) for c in range(4)])
    kcols = OFF_K + np.arange(128)
    krcols = np.concatenate([OFF_K + pt, OFF_K + 64 + pt])
    vcols = OFF_V + np.arange(128)
    ab, ac, ax = np.arange(0, 256), np.arange(256, 512), np.arange(512, 768)
    u, sv = OFF_U + np.arange(256), OFF_SV + np.arange(256)
    colsA = np.concatenate([qcols, qrcols, kcols, krcols, vcols, ac, ax])
    colsC2 = np.concatenate([ab, ac, ax, u, sv])
    brow = np.concatenate([np.concatenate([c * 64 + np.arange(64), (4 + c) * 64 + np.arange(64)]) for c in range(4)])
    out = {k: [] for k in WSPEC}
    for l in range(L):
        out["WA"].append(_unit_layout(w_in[l][:, colsA]))
        out["WC2"].append(_unit_layout(w_in[l][:, colsC2]))
        wm = np.concatenate([np.asarray(inp["w_a"][l], f), np.asarray(inp["w_b"][l], f)[brow], np.asarray(inp["w_c"][l], f),
                             w_in[l][:, OFF_G:OFF_G + 1024], w_in[l][:, OFF_G + 1024:OFF_G + 2048], w_in[l][:, OFF_G + 2048:OFF_G + 3072]], 0)
        out["WM"].append(_unit_layout(wm))
        out["WO"].append(_unit_layout(np.asarray(inp["w_o"][l], f)))
        w1, w3 = np.asarray(inp["w_ff1"][l], f), np.asarray(inp["w_ff3"][l], f)
        w13 = np.stack([w1.reshape(D, 22, 128), w3.reshape(D, 22, 128)], 2).reshape(D, 44 * 128)
        out["W13"].append(_unit_layout(w13))
        out["W2"].append(_unit_layout(np.asarray(inp["w_ff2"][l], f)))
    sh = {k: np.stack(v) for k, v in out.items()}
    p = np.arange(128)
    sh["_wmod"] = np.asarray(inp["w_mod"], f)
    sh["_bmodT"] = np.ascontiguousarray(np.asarray(inp["b_mod"], f).reshape(L, 48, 128).transpose(2, 0, 1))
    n12 = np.stack([np.asarray(inp["norm1"], f), np.asarray(inp["norm2"], f)], 1)
    sh["n12T"] = np.ascontiguousarray(n12.reshape(L, 2, 8, 128).transpose(3, 0, 1, 2))
    qg, kg = np.asarray(inp["q_gain"], f), np.asarray(inp["k_gain"], f)
    d = p % 64
    sh["gains"] = np.ascontiguousarray(np.stack([qg[:, d], qg[:, pt[d]], kg[:, d], kg[:, pt[d]]], -1).transpose(1, 0, 2))
    cw = np.asarray(inp["conv_w"], f)
    sh["convw"] = np.ascontiguousarray(cw.reshape(L, 3, 2, 128).transpose(3, 0, 2, 1))
    sh["sgn"] = np.ascontiguousarray(np.broadcast_to(np.asarray(inp["sg_norm"], f)[None], (128, L, 256)))
    bs = np.asarray(inp["b_s"], f)
    sh["bsT"] = np.ascontiguousarray(bs.reshape(L, 2, 2, 128)[:, :, p // 64, :].transpose(2, 0, 1, 3))
    ws = np.asarray(inp["w_s"], f)
    sh["wsT"] = np.ascontiguousarray(ws.transpose(3, 0, 1, 2))
    return sh


def _rope_tabs(j):
    f = np.float32
    n = j * NLAT + np.arange(NLAT)
    row = (n // 64).astype(f)
    col = (n % 64).astype(f)
    inv = (np.float32(10000.0) ** (-np.arange(0, 32, 2, dtype=f) / np.float32(32))).astype(f)
    d = np.arange(128) % 64
    axis, half, fr = d // 32, (d % 32) // 16, d % 16
    pos = np.where(axis[:, None] == 0, row[None, :], col[None, :]).astype(f)
    ang = (pos * inv[fr][:, None]).astype(f)
    cos = np.cos(ang).astype(f)
    sin = np.sin(ang).astype(f)
    sin = np.where(half[:, None] == 0, -sin, sin).astype(f)
    return np.ascontiguousarray(np.stack([cos, sin], 1))


def make_in_maps(inp):
    f = np.float32
    sh = _prep_shared(inp)
    x, ctx, c, c_ctx = (np.asarray(inp[k], f) for k in ("x", "ctx", "c", "c_ctx"))
    maps = []
    for r in range(8):
        b, j = r // 4, r % 4
        xt = np.concatenate([x[b, j * NLAT:(j + 1) * NLAT], ctx[b]], 0)
        m = {k: v for k, v in sh.items() if not k.startswith("_")}
        m["wmod"] = np.ascontiguousarray(sh["_wmod"][:, :, 1536 * j:1536 * (j + 1)])
        m["bmodT"] = np.ascontiguousarray(sh["_bmodT"][:, :, 12 * j:12 * (j + 1)])
        m["xin"] = np.ascontiguousarray(xt.reshape(NT, 8, 128).transpose(2, 1, 0))
        m["cT"] = np.ascontiguousarray(np.stack([c[b], c_ctx], -1).reshape(8, 128, 2).transpose(1, 0, 2))
        m["tabs"] = _rope_tabs(j)
        mk = np.zeros((128, 2, 4), f)
        if j > 0:
            mk[:, 0, j - 1] = 1
        if j < 3:
            mk[:, 1, j + 1] = 1
        m["masks"] = mk
        maps.append(m)
    return maps


def gather_out(results):
    out = np.zeros((2, 8192, D), np.float32)
    for r in range(8):
        b, j = r // 4, r % 4
        xo = np.asarray(results[r]["xout"])
        out[b, j * NLAT:(j + 1) * NLAT] = xo.transpose(2, 1, 0).reshape(NLAT, D)
    return out


SETUP_KEYS = ["xin", "cT", "wmod", "bmodT", "n12T"]
COMMON_KEYS = ["gains", "convw", "sgn", "bsT", "wsT", "tabs", "masks"]


def _launch(seq, maps, state, nlast=L):
    nc = build(seq, nlast)
    has_setup = seq == "fused" or "setup" in seq
    wk = set()
    if seq == "fused":
        wk.update(WSPEC.keys())
    for t in seq:
        if t[0] == "A":
            wk.add("WA")
        if t[0] == "C":
            wk.update(["WC2", "WM", "WO", "W13", "W2"])
    ims = []
    for r in range(8):
        m = {k: maps[r][k] for k in COMMON_KEYS}
        for k in wk:
            m[k] = maps[r][k]
        if has_setup:
            for k in SETUP_KEYS:
                m[k] = maps[r][k]
        else:
            for k, v in state[r].items():
                m["st_" + k] = v
        ims.append(m)
    res = run_bass_kernel_spmd(nc, ims, core_ids=list(range(8)))
    return res.results


def _exchange_host(results):
    state = []
    for r in range(8):
        g = (r // 4) * 4
        o = results[r]
        ar = np.array(o["o_ARENA"])
        snds = [np.asarray(results[g + j]["o_SND"]) for j in range(4)]
        ar[:, 0:8192] = np.concatenate([sd[:, 0:2048] for sd in snds], 1)
        ar[:, 8448:8448 + 12288] = np.concatenate([sd[:, 2048:5120] for sd in snds], 1)
        halo = np.stack([sd[:, 5120:5128] for sd in snds], 1)
        state.append({"X": np.asarray(o["o_X"]), "QA": np.asarray(o["o_QA"]), "ARENA": ar, "ZB": np.asarray(o["o_ZB"]),
                      "modT": np.asarray(o["o_modT"]), "SB12": np.asarray(o["o_SB12"]), "HALO": np.ascontiguousarray(halo)})
    return state


def kernel_multi(inputs, nlast=L):
    maps = make_in_maps(inputs)
    seqs = [["setup", "A0"]] + [[f"B{l}", f"C{l}", f"A{l + 1}"] for l in range(nlast - 1)] + [[f"B{nlast - 1}", f"C{nlast - 1}"]]
    state = None
    for seq in seqs:
        results = _launch(seq, maps, state, nlast)
        if seq is not seqs[-1]:
            state = _exchange_host(results)
    return gather_out(results)


def kernel_fused(inputs, nlast=L):
    maps = make_in_maps(inputs)
    results = _launch("fused", maps, None, nlast)
    return gather_out(results)


def kernel(**inputs):
    return kernel_fused(inputs)
```
